# Optimizing a Trainium2 kernel written in Bass

```python
import jax, jax.numpy as jnp
from jax import lax
import numpy as np

D_MODEL = 1024
BATCH = 16
SEQ = 2048
DEPTH = 4

N_META = 16
GLA_HEADS = 4
GLA_K = D_MODEL // 2
GLA_V = D_MODEL
GLA_DK = GLA_K // GLA_HEADS
GLA_DV = GLA_V // GLA_HEADS
GATE_RANK = 16
GATE_TAU = 16.0
CHUNK = 64
CONV_DIM = D_MODEL
CONV_GROUPS = 8
CONV_K = 3
EPS = 1e-6
IN_SIZES = (GLA_K, GLA_K, GLA_V, GLA_V, GATE_RANK,
            CONV_DIM, CONV_DIM, CONV_DIM, CONV_DIM,
            D_MODEL, D_MODEL)
N_IN = GLA_K * 2 + GLA_V * 2 + GATE_RANK + CONV_DIM * 4 + D_MODEL * 2

kernel_name = "hybrid_gla_shortconv_gated_merge"


def rmsnorm(x, g):
    x32 = x.astype(jnp.float32)
    y = x32 * lax.rsqrt(jnp.mean(x32 * x32, axis=-1, keepdims=True) + EPS) * g.astype(jnp.float32)
    return y.astype(x.dtype)


def gla_chunked(q, k, v, g_log):
    bsz, L, H, DK = q.shape
    DV = v.shape[-1]
    front = (-N_META) % CHUNK
    back = (-(L + front)) % CHUNK
    Lp = L + front + back
    n_chunks = Lp // CHUNK

    def to_chunks(t):
        t = jnp.pad(t.astype(jnp.float32), ((0, 0), (front, back), (0, 0), (0, 0)))
        return t.reshape(bsz, n_chunks, CHUNK, H, t.shape[-1]).transpose(0, 3, 1, 2, 4)

    q, k, v, g_log = to_chunks(q), to_chunks(k), to_chunks(v), to_chunks(g_log)
    b = jnp.cumsum(g_log, axis=3)
    b_last = b[:, :, :, -1:, :]
    q_in = q * jnp.exp(b)
    k_in = k * jnp.exp(-b)
    k_st = k * jnp.exp(b_last - b)

    causal = jnp.tril(jnp.ones((CHUNK, CHUNK), dtype=bool))
    att = jnp.einsum('bhncd,bhnsd->bhncs', q_in, k_in)
    att = jnp.where(causal, att, 0.0)
    o_intra = jnp.einsum('bhncs,bhnse->bhnce', att, v)

    decay = jnp.exp(b_last[:, :, :, 0, :])

    def step(state, xs):
        q_n, k_n, v_n, d_n = xs
        o_n = jnp.einsum('bhcd,bhde->bhce', q_n, state)
        state = state * d_n[..., None] + jnp.einsum('bhcd,bhce->bhde', k_n, v_n)
        return state, o_n

    s0 = jnp.zeros((bsz, H, DK, DV), jnp.float32)
    xs = (jnp.moveaxis(q_in, 2, 0), jnp.moveaxis(k_st, 2, 0), jnp.moveaxis(v, 2, 0), jnp.moveaxis(decay, 2, 0))
    _, o_inter = lax.scan(step, s0, xs)
    o = o_intra + jnp.moveaxis(o_inter, 0, 2)
    o = o.transpose(0, 2, 3, 1, 4).reshape(bsz, Lp, H, DV)
    return o[:, front:front + L]


def causal_dwconv(u, w):
    return lax.conv_general_dilated(
        u, w[:, None, :].astype(u.dtype), window_strides=(1,), padding=[(CONV_K - 1, 0)],
        dimension_numbers=('NWC', 'WIO', 'NWC'), feature_group_count=u.shape[-1])


def hybrid_layer(x, norm_g, w_in, w_gate_up, b_gate, gla_norm_g, w_o_gla, conv_w, w_o_conv, w_out):
    bsz, L, _ = x.shape
    h = rmsnorm(x, norm_g)
    p = h @ w_in
    offsets = [int(o) for o in np.cumsum(IN_SIZES)[:-1]]
    q, k, v, r, glr, ch, cb, cc, cz, ga, gb = jnp.split(p, offsets, axis=-1)

    g_log = jax.nn.log_sigmoid((glr @ w_gate_up + b_gate).astype(jnp.float32)) / GATE_TAU
    q = q.reshape(bsz, L, GLA_HEADS, GLA_DK) * (GLA_DK ** -0.5)
    k = k.reshape(bsz, L, GLA_HEADS, GLA_DK)
    v = v.reshape(bsz, L, GLA_HEADS, GLA_DV)
    g_log = g_log.reshape(bsz, L, GLA_HEADS, GLA_DK)
    o = gla_chunked(q, k, v, g_log)
    o = rmsnorm(o, gla_norm_g.reshape(GLA_HEADS, GLA_DV)).astype(x.dtype)
    o = o.reshape(bsz, L, GLA_V) * jax.nn.silu(r)
    y_gla = o @ w_o_gla

    y_c = cb * causal_dwconv(cc * ch, conv_w)
    y_c = y_c * jax.nn.silu(cz)
    y_conv = y_c @ w_o_conv

    merged = jax.nn.sigmoid(ga) * y_gla + jax.nn.sigmoid(gb) * y_conv
    return x + merged @ w_out


def setup_inputs(seed: int = 0) -> dict:
    key = jax.random.key(seed)
    ks = jax.random.split(key, 13)
    f32 = jnp.float32
    return {
        "x": jax.random.normal(ks[0], (BATCH, SEQ, D_MODEL), f32),
        "meta": jax.random.normal(ks[1], (N_META, D_MODEL), f32),
        "norm_g": 1.0 + 0.02 * jax.random.normal(ks[2], (DEPTH, D_MODEL), f32),
        "w_in": jax.random.normal(ks[3], (DEPTH, D_MODEL, N_IN), f32) * D_MODEL ** -0.5,
        "w_gate_up": jax.random.normal(ks[4], (DEPTH, GATE_RANK, GLA_K), f32) * GATE_RANK ** -0.5,
        "b_gate": 0.1 * jax.random.normal(ks[5], (DEPTH, GLA_K), f32),
        "gla_norm_g": 1.0 + 0.02 * jax.random.normal(ks[6], (DEPTH, GLA_V), f32),
        "w_o_gla": jax.random.normal(ks[7], (DEPTH, GLA_V, D_MODEL), f32) * GLA_V ** -0.5,
        "conv_w": jax.random.normal(ks[8], (DEPTH, CONV_K, CONV_DIM), f32) * CONV_K ** -0.5,
        "w_o_conv": jax.random.normal(ks[9], (DEPTH, CONV_DIM, D_MODEL), f32) * CONV_DIM ** -0.5,
        "w_out": jax.random.normal(ks[10], (DEPTH, D_MODEL, D_MODEL), f32) * D_MODEL ** -0.5,
        "final_norm_g": 1.0 + 0.02 * jax.random.normal(ks[11], (D_MODEL,), f32),
    }


def reference(x, meta, norm_g, w_in, w_gate_up, b_gate, gla_norm_g, w_o_gla, conv_w, w_o_conv, w_out, final_norm_g):
    bsz = x.shape[0]
    meta_b = jnp.broadcast_to(meta.astype(x.dtype)[None], (bsz, N_META, D_MODEL))
    h = jnp.concatenate([meta_b, x], axis=1)
    for l in range(DEPTH):
        h = hybrid_layer(h, norm_g[l], w_in[l], w_gate_up[l], b_gate[l], gla_norm_g[l],
                         w_o_gla[l], conv_w[l], w_o_conv[l], w_out[l])
    return rmsnorm(h, final_norm_g)[:, N_META:]
```

```python
import numpy as np
from contextlib import ExitStack
import concourse.bass as bass
import concourse.mybir as mybir
from concourse.bass_utils import run_bass_kernel_spmd

F32 = mybir.dt.float32
BF16 = mybir.dt.bfloat16
AF = mybir.ActivationFunctionType
ALU = mybir.AluOpType

D = 1024
NIN = 9232
NMETA = 16
TT = 512
EPS = 1e-6
QSCALE = 128 ** -0.5
O_Q, O_K, O_V, O_R, O_GLR, O_CH, O_CB, O_CC, O_CZ, O_GA, O_GB = 0, 512, 1024, 2048, 3072, 3088, 4112, 5136, 6160, 7184, 8208

SB = []
SB.append(("k", [("w_in", O_K, 512)]))
SB.append(("v0", [("w_in", O_V, 512)]))
SB.append(("q", [("w_in", O_Q, 512)]))
SB.append(("v1", [("w_in", O_V + 512, 512)]))
for _j in range(8):
    SB.append(("cv%d" % _j, [("w_in", O_CH + 128 * _j, 128), ("w_in", O_CC + 128 * _j, 128),
                             ("w_in", O_CB + 128 * _j, 128), ("w_in", O_CZ + 128 * _j, 128)]))
SB.append(("r0", [("w_in", O_R, 512)]))
SB.append(("r1", [("w_in", O_R + 512, 512)]))
SB.append(("wog0", [("w_o_gla", 0, 512)]))
SB.append(("wog1", [("w_o_gla", 512, 512)]))
SB.append(("ga0", [("w_in", O_GA, 512)]))
SB.append(("ga1", [("w_in", O_GA + 512, 512)]))
SB.append(("gb0", [("w_in", O_GB, 512)]))
SB.append(("gb1", [("w_in", O_GB + 512, 512)]))
SB.append(("woc0", [("w_o_conv", 0, 512)]))
SB.append(("woc1", [("w_o_conv", 512, 512)]))
SB.append(("wout0", [("w_out", 0, 512)]))
SB.append(("wout1", [("w_out", 512, 512)]))
NSB = len(SB)
SBI = {n: i for i, (n, _) in enumerate(SB)}
NW = 4


class Op:
    __slots__ = ("eng", "fn", "deps", "chan", "chan_val", "needs_sig", "sig", "idx")


class Prog:
    ENGS = ("pe", "act", "dve", "pool", "sp")

    def __init__(self):
        self.ops = {e: [] for e in self.ENGS}
        self.lastw = {}
        self.readers = {}
        self.chan_cnt = {}
        self.n = 0

    def add(self, eng, fn, reads=(), writes=(), chan=None, extra=()):
        o = Op()
        o.eng, o.fn, o.chan, o.needs_sig, o.sig, o.idx = eng, fn, chan, False, None, self.n
        self.n += 1
        o.chan_val = None
        if chan is not None:
            self.chan_cnt[chan] = self.chan_cnt.get(chan, 0) + 16
            o.chan_val = self.chan_cnt[chan]
        deps = {}
        for k in reads:
            w = self.lastw.get(k)
            if w is not None:
                deps[w.idx] = w
        for k in writes:
            w = self.lastw.get(k)
            if w is not None:
                deps[w.idx] = w
            for r in self.readers.get(k, {}).values():
                deps[r.idx] = r
        for x in extra:
            deps[x.idx] = x
        deps.pop(o.idx, None)
        o.deps = list(deps.values())
        for k in reads:
            self.readers.setdefault(k, {})[(eng, chan)] = o
        for k in writes:
            self.lastw[k] = o
            self.readers[k] = {}
        self.ops[eng].append(o)
        return o

    def emit(self, nc, stack, final_waits):
        for e in self.ENGS:
            for o in self.ops[e]:
                for p in o.deps:
                    if p.chan is None and not (p.eng == "pe" and o.eng == "pe" and o.chan is None):
                        p.needs_sig = True
        esem = {e: stack.enter_context(nc.semaphore("sem_" + e)) for e in self.ENGS}
        csem = {c: stack.enter_context(nc.semaphore("ch_" + c)) for c in self.chan_cnt}
        for e in self.ENGS:
            cnt = 0
            for o in self.ops[e]:
                if o.needs_sig:
                    cnt += 1
                    o.sig = cnt
        block = stack.enter_context(nc.Block())
        prog = self

        def run(ename, eh):
            waited = {}
            for o in prog.ops[ename]:
                for p in o.deps:
                    if p.chan is not None:
                        s, v = csem[p.chan], p.chan_val
                    else:
                        if p.eng == "pe" and o.eng == "pe" and o.chan is None:
                            continue
                        s, v = esem[p.eng], p.sig
                    key = id(s)
                    if waited.get(key, 0) >= v:
                        continue
                    waited[key] = v
                    eh.wait_ge(s, v)
                ins = o.fn(eh)
                if o.chan is not None:
                    ins.then_inc(csem[o.chan], 16)
                elif o.needs_sig:
                    ins.then_inc(esem[ename], 1)
            for c in final_waits.get(ename, ()):
                if c in csem:
                    eh.wait_ge(csem[c], prog.chan_cnt[c])

        @block.tensor
        def _(e):
            run("pe", e)

        @block.scalar
        def _(e):
            run("act", e)

        @block.vector
        def _(e):
            run("dve", e)

        @block.gpsimd
        def _(e):
            run("pool", e)

        @block.sync
        def _(e):
            run("sp", e)


def build_nc(depth, nseq, ntile, with_meta=True):
    nc = bass.Bass("TRN2", target_bir_lowering=False, dynamic_dma_scratch_size=8192)
    st = ExitStack()
    P = Prog()
    S = nseq * ntile * TT

    def dram(name, shape, dt=F32, kind="ExternalInput"):
        return nc.dram_tensor(name, list(shape), dt, kind=kind).ap()

    x_d = dram("x", [nseq, ntile * TT, D])
    meta_d = dram("meta", [NMETA, D])
    normg_d = dram("norm_g", [depth, D])
    win_d = dram("w_in", [depth, D, NIN])
    wgu_d = dram("w_gate_up", [depth, 16, 512])
    bg_d = dram("b_gate", [depth, 512])
    glag_d = dram("gla_norm_g", [depth, D])
    wog_d = dram("w_o_gla", [depth, D, D])
    convw_d = dram("conv_w", [depth, 3, D])
    woc_d = dram("w_o_conv", [depth, D, D])
    wout_d = dram("w_out", [depth, D, D])
    fng_d = dram("final_norm_g", [D])
    out_d = dram("out", [nseq, ntile * TT, D], kind="ExternalOutput")
    wb_d = dram("wb_scr", [depth, NSB, 128, 4096], BF16, kind="Internal")
    wbg_d = dram("wbg_scr", [depth, 128, 1024], BF16, kind="Internal")
    wsrc = {"w_in": win_d, "w_o_gla": wog_d, "w_o_conv": woc_d, "w_out": wout_d}

    def sb(name, shape, dt=F32):
        return st.enter_context(nc.sbuf_tensor(name, list(shape), dt))

    xT = sb("xT", [128, 8, TT])
    xTm = sb("xTm", [128, 8, NMETA])
    xin = [sb("xin%d" % i, [128, 512]) for i in range(2)]
    hT = sb("hT", [128, 8, TT], BF16)
    sqt = [sb("sqt%d" % i, [128, TT], BF16) for i in range(2)]
    rstd = sb("rstd", [128, TT])
    glr = sb("glr", [128, TT], BF16)
    scr = sb("scr", [128, 8, TT])
    ebt = [sb("ebt%d" % i, [128, TT]) for i in range(2)]
    enbt = [sb("enbt%d" % i, [128, TT]) for i in range(2)]
    kstT = [sb("kstT%d" % i, [128, TT]) for i in range(2)]
    dec = sb("dec", [128, 4, 4])
    QK = sb("QK", [128, 8, TT], BF16)
    kst = sb("kst", [128, 4, TT], BF16)
    VY = sb("VY", [128, 8, TT], BF16)
    att = [sb("att%d" % i, [128, 4, 128], BF16) for i in range(2)]
    oT = sb("oT", [128, 8, TT])
    tmpa = [sb("tmpa%d" % i, [128, TT]) for i in range(2)]
    osq = sb("osq", [128, 8, TT], BF16)
    Sst = sb("Sst", [128, depth, 4, 256])
    Smeta = sb("Smeta", [128, depth, 4, 256])
    Sbf = [sb("Sbf%d" % i, [128, 4, 256], BF16) for i in range(2)]
    ycT = sb("ycT", [128, 8, TT], BF16)
    tail = sb("tail", [128, depth, 8, 2])
    tailm = sb("tailm", [128, depth, 8, 2])
    chs = [sb("chs%d" % i, [128, TT]) for i in range(2)]
    ubuf = [sb("ubuf%d" % i, [128, TT + 2]) for i in range(2)]
    cvb = [sb("cvb%d" % i, [128, TT]) for i in range(2)]
    wring = [sb("wring%d" % i, [128, 8, 512], BF16) for i in range(NW)]
    wgl = [sb("wgl%d" % i, [128, 8, 128], BF16) for i in range(1)]
    ones_f = sb("ones_f", [128, 512])
    ones_b = sb("ones_b", [128, 128], BF16)
    ident = sb("ident", [128, 128])
    tri_i = sb("tri_i", [128, 128])
    tri_s = sb("tri_s", [128, 128])
    mask4 = sb("mask4", [128, 4, 128])
    cst_in = [sb("cst_in%d" % i, [128, 128]) for i in range(2)]
    cst = sb("cst", [128, 256])
    wgu1 = sb("wgu1", [128, depth, 512], BF16)
    epst = sb("epst", [128, 1])
    junk = sb("junk", [128, 1])
    ps = st.enter_context(nc.psum_tensor("ps", [128, 8, 512], F32))

    psn = {"c": 0, "f": 0}

    def bank(n=1, ring="c"):
        if ring == "f":
            b = 6 + psn["f"] % 2
            psn["f"] += 1
            return b
        if n == 2 and psn["c"] % 2:
            psn["c"] += 1
        b = psn["c"] % 6
        psn["c"] += n
        return b

    def PK(b, n=1):
        return [("ps", b + i) for i in range(n)]

    P.add("pool", lambda e: e.memset(ones_f[:], 1.0), writes=["ones_f"])
    P.add("pool", lambda e: e.memset(ones_b[:], 1.0), writes=["ones_b"])
    P.add("pool", lambda e: e.memset(epst[:], EPS), writes=["eps"])
    P.add("pool", lambda e: e.affine_select(out=tri_i[:], in_=ones_f[:, 0:128], pattern=[[1, 128]], compare_op=ALU.is_ge,
                                            fill=0.0, base=0, channel_multiplier=-1), reads=["ones_f"], writes=["tri_i"])
    P.add("pool", lambda e: e.affine_select(out=tri_s[:], in_=ones_f[:, 0:128], pattern=[[-1, 128]], compare_op=ALU.is_gt,
                                            fill=0.0, base=0, channel_multiplier=1), reads=["ones_f"], writes=["tri_s"])
    P.add("pool", lambda e: e.affine_select(out=ident[:], in_=ones_f[:, 0:128], pattern=[[1, 128]], compare_op=ALU.is_equal,
                                            fill=0.0, base=0, channel_multiplier=-1), reads=["ones_f"], writes=["ident"])
    P.add("pool", lambda e: e.affine_select(out=mask4[:], in_=ones_f[:].rearrange("p (a b) -> p a b", a=4),
                                            pattern=[[0, 4], [1, 128]], compare_op=ALU.is_ge, fill=0.0, base=0,
                                            channel_multiplier=-1), reads=["ones_f"], writes=["mask4"])
    P.add("pool", lambda e: e.memset(cst_in[0][:], 0.0), writes=["cst_in0"])
    P.add("pool", lambda e: e.memset(cst_in[1][:], 0.0), writes=["cst_in1"])
    P.add("pool", lambda e: e.memset(wgu1[:], 0.0), writes=["wgu1"])
    P.add("pool", lambda e: e.memset(glr[:], 1.0), writes=["glr"])
    P.add("pool", lambda e: e.memset(tail[:], 0.0), writes=[("tl", l, j) for l in range(depth) for j in range(8)])
    P.add("pool", lambda e: e.memset(Sst[:], 0.0), writes=[("S", l) for l in range(depth)])
    P.add("sp", lambda e: e.dma_start(out=cst_in[0][0:depth * 8, :], in_=normg_d.rearrange("l (j p) -> (l j) p", p=128)),
          reads=[], writes=["cst_in0"], chan="const")
    P.add("sp", lambda e: e.dma_start(out=cst_in[0][32:32 + depth * 8, :], in_=glag_d.rearrange("l (j p) -> (l j) p", p=128)),
          writes=["cst_in0"], chan="const")
    P.add("sp", lambda e: e.dma_start(out=cst_in[0][64:72, :], in_=fng_d.rearrange("(j p) -> j p", p=128)),
          writes=["cst_in0"], chan="const")
    P.add("sp", lambda e: e.dma_start(out=cst_in[1][0:depth * 24, :], in_=convw_d.rearrange("l k (j p) -> (l k j) p", p=128)),
          writes=["cst_in1"], chan="constb")
    for i in range(2):
        b = bank()
        P.add("pe", lambda e, i=i, b=b: e.transpose(out=ps[:, b, 0:128], in_=cst_in[i][:], identity=ident[:]),
              reads=["cst_in%d" % i, "ident"], writes=PK(b))
        P.add("dve", lambda e, i=i, b=b: e.tensor_copy(out=cst[:, i * 128:(i + 1) * 128], in_=ps[:, b, 0:128]),
              reads=PK(b), writes=["cst"])
    for l in range(depth):
        P.add("pool", lambda e, l=l: e.dma_start(out=wgu1[0:16, l, :], in_=wgu_d[l]), writes=["wgu1"], chan="const2")
        P.add("pool", lambda e, l=l: e.dma_start(out=wgu1[16:17, l, :], in_=bg_d[l:l + 1, :]), writes=["wgu1"], chan="const2")

    def ng(l, j):
        return cst[:, l * 8 + j:l * 8 + j + 1]

    def gg(l, j):
        return cst[:, 32 + l * 8 + j:32 + l * 8 + j + 1]

    def fg(j):
        return cst[:, 64 + j:64 + j + 1]

    def cw(l, k, j):
        c = 128 + l * 24 + k * 8 + j
        return cst[:, c:c + 1]

    cast_list = {l: [] for l in range(depth)}
    cast_last = {}
    NCC = 7
    cast_hist = []

    def cast_add(l, fn, key):
        i = len(cast_hist)
        extra = [cast_hist[i - NCC]] if i >= NCC else []
        o = P.add("pool", fn, writes=[key], chan="cast%d" % (i % NCC), extra=extra)
        cast_hist.append(o)
        cast_list[l].append(o)

    def emit_casts(l):
        cast_add(l, lambda e, l=l: e.dma_start(
            out=wbg_d[l].rearrange("p (c n) -> p c n", n=128),
            in_=win_d[l].rearrange("(c p) n -> p c n", p=128)[:, :, O_GLR:O_GLR + 128]), ("wbgc", l))
        cast_last[(l, "glr")] = 0
        for s, (name, pieces) in enumerate(SB):
            off = 0
            for pi, (src, c0, ncol) in enumerate(pieces):
                cast_add(l, lambda e, l=l, s=s, src=src, c0=c0, ncol=ncol, off=off: e.dma_start(
                    out=wb_d[l, s].rearrange("p (c n) -> p c n", n=512)[:, :, off:off + ncol],
                    in_=wsrc[src][l].rearrange("(c p) n -> p c n", p=128)[:, :, c0:c0 + ncol]), ("wbc", l, s, pi))
                off += ncol
            cast_last[(l, name)] = len(cast_list[l]) - 1

    cast_first = {}

    def cast_dep(l, name):
        lst = cast_list[l]
        last = cast_last[(l, name)]
        names = ["glr"] + [n for n, _ in SB]
        prev = names.index(name) - 1
        first = 0 if prev < 0 else cast_last[(l, names[prev])] + 1
        return lst[first:last + 1]

    for l in range(depth):
        emit_casts(l)

    wcnt = [0]

    def load_sb(l, name):
        s = SBI[name]
        slot = wcnt[0] % NW
        wcnt[0] += 1
        P.add("sp", lambda e, l=l, s=s, slot=slot: e.dma_start(
            out=wring[slot][:].rearrange("p c n -> p (c n)"), in_=wb_d[l, s]),
            writes=[("wr", slot)], chan="w%d" % slot, extra=cast_dep(l, name))
        return slot

    gcnt = [0]

    def load_glr(l):
        slot = 0
        P.add("sp", lambda e, l=l, slot=slot: e.dma_start(
            out=wgl[slot][:].rearrange("p c n -> p (c n)"), in_=wbg_d[l]),
            writes=[("wg", slot)], chan="wg%d" % slot, extra=cast_dep(l, "glr"))
        return slot

    def proj_fm(slot, m, src, skeys, T, b):
        for c in range(8):
            P.add("pe", lambda e, c=c: e.matmul(ps[:, b, 0:T], lhsT=wring[slot][:, c, m * 128:(m + 1) * 128],
                                               rhs=src[:, c, 0:T], start=(c == 0), stop=(c == 7)),
                  reads=[("wr", slot)] + ([skeys[c]] if c < len(skeys) else []) + (skeys if c == 7 else []),
                  writes=PK(b))

    def rms_stat(src, skeys, T, nch, scale, out_ap, okey, tkeys):
        b = bank()
        for i, (a, k) in enumerate(zip(src, skeys)):
            t = i % 2
            P.add("act", lambda e, a=a, t=t: e.activation(out=sqt[t][:, 0:T], in_=a, func=AF.Square),
                  reads=[k], writes=[("sqt", t)])
            P.add("pe", lambda e, t=t, i=i: e.matmul(ps[:, b, 0:T], lhsT=ones_b[:], rhs=sqt[t][:, 0:T],
                                                   start=(i == 0), stop=(i == nch - 1)),
                  reads=[("sqt", t), "ones_b"], writes=PK(b))
        P.add("act", lambda e: e.activation(out=out_ap, in_=ps[:, b, 0:T], func=AF.Ln, scale=scale, bias=epst[:]),
              reads=PK(b) + ["eps"], writes=[okey])
        P.add("act", lambda e: e.activation(out=out_ap, in_=out_ap, func=AF.Exp, scale=-0.5),
              reads=[okey], writes=[okey])

    rstd_owner = [None]

    def tile_layer(l, X, T, first_of_seq, is_meta, use_meta_state):
        chunks = [(o, min(128, T - o)) for o in range(0, T, 128)]
        nchk = len(chunks)
        XK = [("xT", j) for j in range(8)]
        HK = [("hT", j) for j in range(8)]

        if first_of_seq and not is_meta:
            if use_meta_state:
                P.add("dve", lambda e: e.tensor_copy(out=Sst[:, l], in_=Smeta[:, l]), reads=[("Sm", l)], writes=[("S", l)])
                P.add("dve", lambda e: e.tensor_copy(out=tail[:, l], in_=tailm[:, l]), reads=[("tlm", l)], writes=[("tl", l, j) for j in range(8)])
            else:
                P.add("dve", lambda e: e.memset(Sst[:, l], 0.0), writes=[("S", l)])
                P.add("dve", lambda e: e.memset(tail[:, l], 0.0), writes=[("tl", l, j) for j in range(8)])
        P.add("act", lambda e: e.activation(out=Sbf[0][:], in_=Sst[:, l], func=AF.Copy), reads=[("S", l)], writes=[("Sbf", 0)])

        if rstd_owner[0] is not X:
            rms_stat([X[:, j, 0:T] for j in range(8)], XK, T, 8, 1.0 / D, rstd[:, 0:T], "rstd", None)
            rstd_owner[0] = X
        for j in range(8):
            P.add("dve", lambda e, j=j: e.scalar_tensor_tensor(out=hT[:, j, 0:T], in0=X[:, j, 0:T], scalar=ng(l, j),
                                                              in1=rstd[:, 0:T], op0=ALU.mult, op1=ALU.mult),
                  reads=[XK[j], "rstd", "cst"], writes=[HK[j]])

        convring = ["f"]

        def conv_gen():
            for j in range(8):
                cs = load_sb(l, "cv%d" % j)
                t = j % 2
                bch = 7 if convring[0] == "f" else bank(1, "c")
                proj_fm(cs, 0, hT, HK, T, bch)
                P.add("act", lambda e, bch=bch, t=t: e.activation(out=chs[t][:, 0:T], in_=ps[:, bch, 0:T], func=AF.Copy),
                      reads=PK(bch), writes=[("chs", t)])
                yield
                bcc = 7 if convring[0] == "f" else bank(1, "c")
                proj_fm(cs, 1, hT, HK, T, bcc)
                P.add("dve", lambda e, j=j, t=t: e.tensor_copy(out=ubuf[t][:, 0:2], in_=tail[:, l, j, :]),
                      reads=[("tl", l, j)], writes=[("ub", t)])
                P.add("dve", lambda e, bcc=bcc, t=t: e.tensor_tensor(out=ubuf[t][:, 2:2 + T], in0=ps[:, bcc, 0:T], in1=chs[t][:, 0:T], op=ALU.mult),
                      reads=PK(bcc) + [("chs", t), ("ub", t)], writes=[("ub", t)])
                P.add("dve", lambda e, j=j, t=t: e.tensor_copy(out=tail[:, l, j, :], in_=ubuf[t][:, T:T + 2]),
                      reads=[("ub", t)], writes=[("tl", l, j)])
                if is_meta:
                    P.add("dve", lambda e, j=j, t=t: e.tensor_copy(out=tailm[:, l, j, :], in_=ubuf[t][:, T:T + 2]),
                          reads=[("ub", t)], writes=[("tlm", l)])
                P.add("dve", lambda e, j=j, t=t: e.tensor_scalar(out=cvb[t][:, 0:T], in0=ubuf[t][:, 2:2 + T], scalar1=cw(l, 2, j), scalar2=None, op0=ALU.mult),
                      reads=[("ub", t), "cst"], writes=[("cvb", t)])
                P.add("dve", lambda e, j=j, t=t: e.scalar_tensor_tensor(out=cvb[t][:, 0:T], in0=ubuf[t][:, 1:1 + T], scalar=cw(l, 1, j),
                                                                       in1=cvb[t][:, 0:T], op0=ALU.mult, op1=ALU.add),
                      reads=[("ub", t), ("cvb", t)], writes=[("cvb", t)])
                P.add("dve", lambda e, j=j, t=t: e.scalar_tensor_tensor(out=cvb[t][:, 0:T], in0=ubuf[t][:, 0:T], scalar=cw(l, 0, j),
                                                                       in1=cvb[t][:, 0:T], op0=ALU.mult, op1=ALU.add),
                      reads=[("ub", t), ("cvb", t)], writes=[("cvb", t)])
                yield
                bcb = 7 if convring[0] == "f" else bank(1, "c")
                proj_fm(cs, 2, hT, HK, T, bcb)
                P.add("dve", lambda e, bcb=bcb, t=t: e.tensor_tensor(out=cvb[t][:, 0:T], in0=ps[:, bcb, 0:T], in1=cvb[t][:, 0:T], op=ALU.mult),
                      reads=PK(bcb) + [("cvb", t)], writes=[("cvb", t)])
                yield
                bcz = 7 if convring[0] == "f" else bank(1, "c")
                proj_fm(cs, 3, hT, HK, T, bcz)
                P.add("act", lambda e, bcz=bcz, t=t: e.activation(out=tmpa[t][:, 0:T], in_=ps[:, bcz, 0:T], func=AF.Silu),
                      reads=PK(bcz), writes=[("tmpa", t)])
                P.add("dve", lambda e, j=j, t=t: e.tensor_tensor(out=ycT[:, j, 0:T], in0=cvb[t][:, 0:T], in1=tmpa[t][:, 0:T], op=ALU.mult),
                      reads=[("cvb", t), ("tmpa", t)], writes=[("yc", j)])
                yield

        filler = conv_gen()

        def fill(n):
            for _ in range(n):
                try:
                    next(filler)
                except StopIteration:
                    return

        def v_chunk(vs, half, ci, o, nt):
            b = bank()
            for c in range(8):
                P.add("pe", lambda e, c=c: e.matmul(ps[0:nt, b, :], lhsT=hT[:, c, o:o + nt], rhs=wring[vs][:, c, :],
                                                   start=(c == 0), stop=(c == 7)),
                      reads=[("wr", vs), HK[c]], writes=PK(b))
            P.add("act", lambda e: e.activation(out=VY[0:nt, 2 * ci + half, :], in_=ps[0:nt, b, :], func=AF.Copy),
                  reads=PK(b), writes=[("VY", 2 * ci + half)])

        gs = load_glr(l)
        ks = load_sb(l, "k")
        v0s = load_sb(l, "v0")
        b = bank()
        for c in range(8):
            P.add("pe", lambda e, c=c, b=b: e.matmul(ps[:, b, 0:T], lhsT=wgl[gs][:, c, :], rhs=hT[:, c, 0:T],
                                                   start=(c == 0), stop=(c == 7)),
                  reads=[("wg", gs), HK[c]] + (HK if c == 7 else []), writes=PK(b))
        P.add("act", lambda e, b=b: e.activation(out=glr[0:16, 0:T], in_=ps[0:16, b, 0:T], func=AF.Copy),
              reads=PK(b), writes=["glr"])
        v_chunk(v0s, 0, 0, chunks[0][0], chunks[0][1])
        for ci, (o, nt) in enumerate(chunks):
            b = bank()
            P.add("pe", lambda e, o=o, nt=nt, b=b: e.matmul(ps[0:nt, b, :], lhsT=glr[:, o:o + nt], rhs=wgu1[:, l, :],
                                                          start=True, stop=True),
                  reads=["glr", "wgu1"], writes=PK(b))
            P.add("act", lambda e, ci=ci, nt=nt, b=b: e.activation(out=scr[0:nt, ci, :], in_=ps[0:nt, b, :], func=AF.Exp, scale=-1.0),
                  reads=PK(b), writes=[("scr", ci)])
            P.add("act", lambda e, ci=ci, nt=nt: e.activation(out=scr[0:nt, ci, :], in_=scr[0:nt, ci, :], func=AF.Ln, scale=1.0, bias=1.0),
                  reads=[("scr", ci)], writes=[("scr", ci)])
        for ci, (o, nt) in list(enumerate(chunks))[1:]:
            v_chunk(v0s, 0, ci, o, nt)
        qs = load_sb(l, "q")
        v1s = load_sb(l, "v1")
        pend_tr = []

        def emit_ktr(h, t):
            bt = bank()
            for ci, (o, nt) in enumerate(chunks):
                P.add("pe", lambda e, ci=ci, o=o, nt=nt: e.transpose(out=ps[0:nt, bt, ci * 128:(ci + 1) * 128], in_=kstT[t][:, o:o + nt],
                                                                   identity=ident[:]),
                      reads=[("kstT", t), "ident"], writes=PK(bt))
            P.add("act", lambda e: e.activation(
                out=kst[:, 0:nchk, h * 128:(h + 1) * 128], in_=ps[:, bt, 0:nchk * 128].rearrange("p (c n) -> p c n", n=128), func=AF.Copy),
                reads=PK(bt), writes=[("kst", ci) for ci in range(nchk)])

        for h in range(4):
            t = h % 2
            b = bank()
            for ci, (o, nt) in enumerate(chunks):
                P.add("pe", lambda e, ci=ci, o=o, nt=nt, b=b, h=h: e.matmul(ps[:, b, o:o + nt], lhsT=scr[0:nt, ci, h * 128:(h + 1) * 128],
                                                                         rhs=tri_i[0:nt, 0:nt], start=True, stop=True),
                      reads=[("scr", ci), "tri_i"], writes=PK(b))
            P.add("act", lambda e, b=b, t=t: e.activation(out=ebt[t][:, 0:T], in_=ps[:, b, 0:T], func=AF.Exp, scale=-1.0 / 16),
                  reads=PK(b), writes=[("eb", t)])
            P.add("act", lambda e, b=b, t=t: e.activation(out=enbt[t][:, 0:T], in_=ps[:, b, 0:T], func=AF.Exp, scale=1.0 / 16),
                  reads=PK(b), writes=[("enb", t)])
            for ci, (o, nt) in enumerate(chunks):
                P.add("dve", lambda e, h=h, ci=ci, o=o, nt=nt, t=t: e.tensor_copy(out=dec[:, h, ci:ci + 1], in_=ebt[t][:, o + nt - 1:o + nt]),
                      reads=[("eb", t)], writes=[("dec", h)])
            bk = bank()
            proj_fm(ks, h, hT, HK, T, bk)
            P.add("dve", lambda e, h=h, bk=bk, t=t: e.tensor_tensor(out=QK[:, 4 + h, 0:T], in0=ps[:, bk, 0:T], in1=enbt[t][:, 0:T], op=ALU.mult),
                  reads=PK(bk) + [("enb", t)], writes=[("QK", 4 + h)])
            if not is_meta:
                for ci, (o, nt) in enumerate(chunks):
                    P.add("dve", lambda e, h=h, bk=bk, t=t, ci=ci, o=o, nt=nt: e.scalar_tensor_tensor(
                        out=kstT[t][:, o:o + nt], in0=ps[:, bk, o:o + nt], scalar=dec[:, h, ci:ci + 1],
                        in1=enbt[t][:, o:o + nt], op0=ALU.mult, op1=ALU.mult),
                        reads=PK(bk) + [("enb", t), ("dec", h)], writes=[("kstT", t)])
                pend_tr.append((h, t))
            bq = bank()
            proj_fm(qs, h, hT, HK, T, bq)
            P.add("dve", lambda e, h=h, bq=bq, t=t: e.scalar_tensor_tensor(out=QK[:, h, 0:T], in0=ps[:, bq, 0:T], scalar=QSCALE,
                                                                        in1=ebt[t][:, 0:T], op0=ALU.mult, op1=ALU.mult),
                  reads=PK(bq) + [("eb", t)], writes=[("QK", h)])
            if h < min(2, nchk):
                v_chunk(v1s, 1, h, chunks[h][0], chunks[h][1])
            if len(pend_tr) > 1:
                emit_ktr(*pend_tr.pop(0))
        if is_meta:
            for ci, (o, nt) in enumerate(chunks):
                b2 = bank()
                for c in range(8):
                    P.add("pe", lambda e, c=c, o=o, nt=nt, b2=b2: e.matmul(ps[0:nt, b2, :], lhsT=hT[:, c, o:o + nt], rhs=wring[ks][:, c, :],
                                                                         start=(c == 0), stop=(c == 7)),
                          reads=[("wr", ks), HK[c]], writes=PK(b2))
                b = bank()
                P.add("pe", lambda e, ci=ci, nt=nt, b=b: e.matmul(ps[0:nt, b, :], lhsT=tri_s[0:nt, 0:nt], rhs=scr[0:nt, ci, :],
                                                                start=True, stop=True),
                      reads=[("scr", ci), "tri_s"], writes=PK(b))
                t = ci % 2
                P.add("act", lambda e, nt=nt, b=b, t=t: e.activation(out=tmpa[t][0:nt, :], in_=ps[0:nt, b, :], func=AF.Exp, scale=-1.0 / 16),
                      reads=PK(b), writes=[("tmpa", t)])
                P.add("dve", lambda e, ci=ci, nt=nt, b2=b2, t=t: e.tensor_tensor(out=kst[0:nt, ci, :], in0=ps[0:nt, b2, :], in1=tmpa[t][0:nt, :], op=ALU.mult),
                      reads=PK(b2) + [("tmpa", t)], writes=[("kst", ci)])
        for ci in range(2, nchk):
            if ci == 3:
                while pend_tr:
                    emit_ktr(*pend_tr.pop(0))
            v_chunk(v1s, 1, ci, chunks[ci][0], chunks[ci][1])
        while pend_tr:
            emit_ktr(*pend_tr.pop(0))

        for ci, (o, nt) in enumerate(chunks):
            a = ci % 2
            bs = 0 if a == 0 else 2
            for h in range(4):
                P.add("pe", lambda e, h=h, ci=ci, nt=nt, bs=bs: e.matmul(
                    ps[:, bs + h // 2, (h % 2) * 256:(h % 2) * 256 + 256], lhsT=kst[0:nt, ci, h * 128:(h + 1) * 128],
                    rhs=VY[0:nt, 2 * ci + h // 2, (h % 2) * 256:(h % 2) * 256 + 256], start=True, stop=True),
                    reads=[("kst", ci), ("VY", 2 * ci + h // 2)], writes=PK(bs, 2))
            ba = 6
            for h in range(4):
                P.add("pe", lambda e, h=h, o=o, nt=nt, ba=ba: e.matmul(ps[0:nt, ba, h * 128:h * 128 + nt], lhsT=QK[:, 4 + h, o:o + nt],
                                                                     rhs=QK[:, h, o:o + nt], start=True, stop=True),
                      reads=[("QK", 4 + h), ("QK", h)], writes=PK(ba))
            P.add("dve", lambda e, nt=nt, ba=ba, a=a: e.tensor_tensor(
                out=att[a][0:nt, :, 0:nt], in0=ps[0:nt, ba, :].rearrange("p (h n) -> p h n", h=4)[:, :, 0:nt],
                in1=mask4[0:nt, :, 0:nt], op=ALU.mult),
                reads=PK(ba) + ["mask4"], writes=[("att", a)])
            fill(1)
            for h in range(4):
                P.add("dve", lambda e, h=h, ci=ci, bs=bs: e.scalar_tensor_tensor(
                    out=Sst[:, l, h, :], in0=Sst[:, l, h, :], scalar=dec[:, h, ci:ci + 1],
                    in1=ps[:, bs + h // 2, (h % 2) * 256:(h % 2) * 256 + 256], op0=ALU.mult, op1=ALU.add),
                    reads=PK(bs, 2) + [("S", l), ("dec", h)], writes=[("S", l)])
            if ci < nchk - 1:
                P.add("act", lambda e, a=a: e.activation(out=Sbf[1 - a][:], in_=Sst[:, l], func=AF.Copy), reads=[("S", l)], writes=[("Sbf", 1 - a)])
            fill(1)
            bo = 4
            for h in range(4):
                for half in range(2):
                    g = 2 * h + half
                    oap = ps[:, bo + g // 4, (g % 4) * 128:(g % 4) * 128 + nt]
                    vap = VY[0:nt, 2 * ci + h // 2, (h % 2) * 256 + half * 128:(h % 2) * 256 + half * 128 + 128]
                    P.add("pe", lambda e, oap=oap, vap=vap, nt=nt, h=h, a=a: e.matmul(oap, lhsT=vap, rhs=att[a][0:nt, h, 0:nt], start=True, stop=False),
                          reads=[("VY", 2 * ci + h // 2), ("att", a)], writes=PK(bo, 2))
                    P.add("pe", lambda e, oap=oap, h=h, half=half, o=o, nt=nt, a=a: e.matmul(oap, lhsT=Sbf[a][:, h, half * 128:(half + 1) * 128],
                                                                                          rhs=QK[:, h, o:o + nt], start=False, stop=True),
                          reads=[("Sbf", a), ("QK", h)], writes=PK(bo, 2))
            P.add("act", lambda e, bo=bo, o=o, nt=nt: e.activation(
                out=osq[:, :, o:o + nt], in_=ps[:, bo:bo + 2, :].rearrange("p b (g n) -> p (b g) n", g=4)[:, :, 0:nt], func=AF.Square),
                reads=PK(bo, 2), writes=[("osq", j) for j in range(8)])
            P.add("act", lambda e, bo=bo, o=o, nt=nt: e.activation(
                out=oT[:, :, o:o + nt], in_=ps[:, bo:bo + 2, :].rearrange("p b (g n) -> p (b g) n", g=4)[:, :, 0:nt], func=AF.Copy),
                reads=PK(bo, 2), writes=[("oT", j) for j in range(8)])
        if is_meta:
            P.add("dve", lambda e: e.tensor_copy(out=Smeta[:, l], in_=Sst[:, l]), reads=[("S", l)], writes=[("Sm", l)])

        OK_ = [("oT", j) for j in range(8)]
        for h in range(4):
            b = bank()
            for half in range(2):
                P.add("pe", lambda e, b=b, h=h, half=half: e.matmul(ps[:, b, 0:T], lhsT=ones_b[:], rhs=osq[:, 2 * h + half, 0:T],
                                                                  start=(half == 0), stop=(half == 1)),
                      reads=[("osq", 2 * h + half), "ones_b"], writes=PK(b))
            P.add("act", lambda e, b=b, h=h: e.activation(out=scr[:, 4 + h, 0:T], in_=ps[:, b, 0:T], func=AF.Ln, scale=1.0 / 256, bias=epst[:]),
                  reads=PK(b) + ["eps"], writes=[("scr", 4 + h)])
        for h in range(4):
            P.add("act", lambda e, h=h: e.activation(out=scr[:, 4 + h, 0:T], in_=scr[:, 4 + h, 0:T], func=AF.Exp, scale=-0.5),
                  reads=[("scr", 4 + h)], writes=[("scr", 4 + h)])
        convring[0] = "c"
        fill(64)
        for half, nm in enumerate(("r0", "r1")):
            rs = load_sb(l, nm)
            for m in range(4):
                j = half * 4 + m
                b = bank()
                proj_fm(rs, m, hT, HK, T, b)
                t = j % 2
                P.add("act", lambda e, b=b, t=t: e.activation(out=tmpa[t][:, 0:T], in_=ps[:, b, 0:T], func=AF.Silu),
                      reads=PK(b), writes=[("tmpa", t)])
                P.add("dve", lambda e, j=j: e.scalar_tensor_tensor(out=oT[:, j, 0:T], in0=oT[:, j, 0:T], scalar=gg(l, j),
                                                                  in1=scr[:, 4 + j // 2, 0:T], op0=ALU.mult, op1=ALU.mult),
                      reads=[OK_[j], ("scr", 4 + j // 2), "cst"], writes=[OK_[j]])
                P.add("dve", lambda e, j=j, t=t: e.tensor_tensor(out=QK[:, j, 0:T], in0=oT[:, j, 0:T], in1=tmpa[t][:, 0:T], op=ALU.mult),
                      reads=[OK_[j], ("tmpa", t)], writes=[("QK", j)])
        QKK = [("QK", j) for j in range(8)]
        for half, nm in enumerate(("wog0", "wog1")):
            ws = load_sb(l, nm)
            for m in range(4):
                j = half * 4 + m
                b = bank()
                proj_fm(ws, m, QK, QKK, T, b)
                P.add("act", lambda e, b=b, j=j: e.activation(out=oT[:, j, 0:T], in_=ps[:, b, 0:T], func=AF.Copy),
                      reads=PK(b), writes=[OK_[j]])
        for half, nm in enumerate(("ga0", "ga1")):
            ws = load_sb(l, nm)
            for m in range(4):
                j = half * 4 + m
                b = bank()
                proj_fm(ws, m, hT, HK, T, b)
                t = j % 2
                P.add("act", lambda e, b=b, t=t: e.activation(out=tmpa[t][:, 0:T], in_=ps[:, b, 0:T], func=AF.Sigmoid),
                      reads=PK(b), writes=[("tmpa", t)])
                P.add("dve", lambda e, j=j, t=t: e.tensor_tensor(out=oT[:, j, 0:T], in0=oT[:, j, 0:T], in1=tmpa[t][:, 0:T], op=ALU.mult),
                      reads=[OK_[j], ("tmpa", t)], writes=[OK_[j]])
        VYK = [("yc", j) for j in range(8)]
        for half, nm in enumerate(("gb0", "gb1")):
            ws = load_sb(l, nm)
            for m in range(4):
                j = half * 4 + m
                b = bank()
                proj_fm(ws, m, hT, HK, T, b)
                P.add("act", lambda e, b=b, j=j: e.activation(out=scr[:, j, 0:T], in_=ps[:, b, 0:T], func=AF.Sigmoid),
                      reads=PK(b), writes=[("scr", j)])
        for half, nm in enumerate(("woc0", "woc1")):
            ws = load_sb(l, nm)
            for m in range(4):
                j = half * 4 + m
                b = bank()
                proj_fm(ws, m, ycT, VYK, T, b)
                P.add("dve", lambda e, b=b, j=j: e.tensor_tensor(out=scr[:, j, 0:T], in0=ps[:, b, 0:T], in1=scr[:, j, 0:T], op=ALU.mult),
                      reads=PK(b) + [("scr", j)], writes=[("scr", j)])
                P.add("dve", lambda e, j=j: e.tensor_tensor(out=hT[:, j, 0:T], in0=scr[:, j, 0:T], in1=oT[:, j, 0:T], op=ALU.add),
                      reads=[("scr", j), OK_[j]], writes=[HK[j]])
        fuse = not is_meta
        if fuse:
            bn = bank(1, "f")
            P.add("act", lambda e: e.activation(out=junk[:], in_=epst[:], func=AF.Ln), reads=["eps"], writes=["junk"])

        def onesmm(i):
            P.add("pe", lambda e: e.matmul(ps[:, bn, 0:T], lhsT=ones_b[:], rhs=sqt[i % 2][:, 0:T], start=(i == 0), stop=(i == 7)),
                  reads=[("sqt", i % 2), "ones_b"], writes=PK(bn))

        prev = None
        for half, nm in enumerate(("wout0", "wout1")):
            ws = load_sb(l, nm)
            for m in range(4):
                j = half * 4 + m
                b = bank()
                proj_fm(ws, m, hT, HK, T, b)
                P.add("dve", lambda e, b=b, j=j: e.tensor_tensor(out=X[:, j, 0:T], in0=X[:, j, 0:T], in1=ps[:, b, 0:T], op=ALU.add),
                      reads=PK(b) + [XK[j]], writes=[XK[j]])
                if fuse:
                    P.add("act", lambda e, j=j: e.activation(out=sqt[j % 2][:, 0:T], in_=X[:, j, 0:T], func=AF.Square),
                          reads=[XK[j]], writes=[("sqt", j % 2)])
                    if prev is not None:
                        onesmm(prev)
                    prev = j
        if fuse:
            onesmm(prev)
            P.add("act", lambda e: e.activation(out=rstd[:, 0:T], in_=ps[:, bn, 0:T], func=AF.Ln, scale=1.0 / D, bias=epst[:]),
                  reads=PK(bn) + ["eps"], writes=["rstd"])
            P.add("act", lambda e: e.activation(out=rstd[:, 0:T], in_=rstd[:, 0:T], func=AF.Exp, scale=-0.5),
                  reads=["rstd"], writes=["rstd"])
            rstd_owner[0] = X

    xcnt = [0]
    ocnt = [0]

    def load_tile(X, src_rows_ap, T, fuse=False):
        if fuse:
            bn = bank(1, "f")

        def emit_ones(o, nt, jj, t):
            for q in range(4):
                P.add("pe", lambda e, q=q: e.matmul(ps[:, bn, o:o + nt], lhsT=ones_b[:], rhs=sqt[t][:, q * 128:q * 128 + nt],
                                                   start=(jj == 0 and q == 0), stop=(jj == 1 and q == 3)),
                      reads=[("sqt", t), "ones_b"], writes=PK(bn))

        prev = None
        idx = 0
        for o in range(0, T, 128):
            nt = min(128, T - o)
            for jj in range(2):
                s = xcnt[0] % 2
                xcnt[0] += 1
                P.add("sp", lambda e, s=s, o=o, nt=nt, jj=jj: e.dma_start(out=xin[s][0:nt, :], in_=src_rows_ap[o:o + nt, jj * 512:(jj + 1) * 512]),
                      writes=[("xin", s)], chan="xin%d" % s)
                b = bank()
                for q in range(4):
                    P.add("pe", lambda e, s=s, nt=nt, q=q, b=b: e.transpose(out=ps[:, b, q * 128:q * 128 + nt], in_=xin[s][0:nt, q * 128:(q + 1) * 128],
                                                                          identity=ident[0:nt, 0:nt]),
                          reads=[("xin", s), "ident"], writes=PK(b))
                P.add("dve", lambda e, jj=jj, o=o, nt=nt, b=b: e.tensor_copy(
                    out=X[:, jj * 4:(jj + 1) * 4, o:o + nt], in_=ps[:, b, :].rearrange("p (q n) -> p q n", q=4)[:, :, 0:nt]),
                    reads=PK(b), writes=[("xT", j) for j in range(jj * 4, jj * 4 + 4)])
                if fuse:
                    t = idx % 2
                    idx += 1
                    P.add("act", lambda e, jj=jj, o=o, nt=nt, t=t: e.activation(
                        out=sqt[t][:, :].rearrange("p (q n) -> p q n", q=4)[:, :, 0:nt], in_=X[:, jj * 4:(jj + 1) * 4, o:o + nt], func=AF.Square),
                        reads=[("xT", j) for j in range(jj * 4, jj * 4 + 4)], writes=[("sqt", t)])
                    if prev is not None:
                        emit_ones(*prev)
                    prev = (o, nt, jj, t)
        if fuse:
            emit_ones(*prev)
            P.add("act", lambda e: e.activation(out=rstd[:, 0:T], in_=ps[:, bn, 0:T], func=AF.Ln, scale=1.0 / D, bias=epst[:]),
                  reads=PK(bn) + ["eps"], writes=["rstd"])
            P.add("act", lambda e: e.activation(out=rstd[:, 0:T], in_=rstd[:, 0:T], func=AF.Exp, scale=-0.5),
                  reads=["rstd"], writes=["rstd"])
            rstd_owner[0] = X

    def store_a(T):
        XK = [("xT", j) for j in range(8)]
        if rstd_owner[0] is not xT:
            rms_stat([xT[:, j, 0:T] for j in range(8)], XK, T, 8, 1.0 / D, rstd[:, 0:T], "rstd", None)
        rstd_owner[0] = None
        for j in range(8):
            P.add("dve", lambda e, j=j: e.scalar_tensor_tensor(out=oT[:, j, 0:T], in0=xT[:, j, 0:T], scalar=fg(j),
                                                              in1=rstd[:, 0:T], op0=ALU.mult, op1=ALU.mult),
                  reads=[XK[j], "rstd", "cst"], writes=[("oT", j)])

    def store_b(dst_rows_ap, T):
        stages = [(chs[0], ("chs", 0)), (chs[1], ("chs", 1)), (cvb[0], ("cvb", 0)), (cvb[1], ("cvb", 1))]
        for o in range(0, T, 128):
            for jj in range(2):
                si = ocnt[0] % 4
                ocnt[0] += 1
                stg, skey = stages[si]
                b = bank()
                for q in range(4):
                    j = jj * 4 + q
                    P.add("pe", lambda e, j=j, q=q, o=o, b=b: e.transpose(out=ps[:, b, q * 128:(q + 1) * 128], in_=oT[:, j, o:o + 128], identity=ident[:]),
                          reads=[("oT", j), "ident"], writes=PK(b))
                P.add("act", lambda e, stg=stg, b=b: e.activation(out=stg[:, :], in_=ps[:, b, :], func=AF.Copy),
                      reads=PK(b), writes=[skey])
                P.add("pool", lambda e, stg=stg, o=o, jj=jj: e.dma_start(out=dst_rows_ap[o:o + 128, jj * 512:(jj + 1) * 512], in_=stg[:, :]),
                      reads=[skey], writes=[], chan="out%d" % si)

    if with_meta:
        load_tile(xTm, meta_d, NMETA)
    tiles = [(sq, ti) for sq in range(nseq) for ti in range(ntile)]
    for n, (sq, ti) in enumerate(tiles):
        if n == 0:
            load_tile(xT, x_d[sq, ti * TT:(ti + 1) * TT, :], TT, fuse=True)
        for l in range(depth):
            if with_meta and sq == 0 and ti == 0:
                tile_layer(l, xTm, NMETA, True, True, False)
            tile_layer(l, xT, TT, ti == 0, False, with_meta)
        store_a(TT)
        if n + 1 < len(tiles):
            sq2, ti2 = tiles[n + 1]
            load_tile(xT, x_d[sq2, ti2 * TT:(ti2 + 1) * TT, :], TT, fuse=True)
        store_b(out_d[sq, ti * TT:(ti + 1) * TT, :], TT)

    P.emit(nc, st, {"pool": ["out0", "out1", "out2", "out3"]})
    st.close()
    return nc


def _run(inputs, depth, nseq_total, ntile, n_cores, with_meta=True):
    nseq = nseq_total // n_cores
    nc = build_nc(depth, nseq, ntile, with_meta)
    f = lambda a: np.ascontiguousarray(np.asarray(a, dtype=np.float32))
    x = f(inputs["x"])
    shared = {k: f(inputs[k]) for k in ("meta", "norm_g", "w_in", "w_gate_up", "b_gate", "gla_norm_g", "w_o_gla",
                                         "conv_w", "w_o_conv", "w_out", "final_norm_g")}
    in_maps = []
    for c in range(n_cores):
        m = dict(shared)
        m["x"] = np.ascontiguousarray(x[c * nseq:(c + 1) * nseq])
        in_maps.append(m)
    res = run_bass_kernel_spmd(nc, in_maps, core_ids=list(range(n_cores)))
    return np.concatenate([np.asarray(r["out"]) for r in res.results], axis=0).astype(np.float32)


def kernel(x, meta, norm_g, w_in, w_gate_up, b_gate, gla_norm_g, w_o_gla, conv_w, w_o_conv, w_out, final_norm_g):
    inputs = dict(x=x, meta=meta, norm_g=norm_g, w_in=w_in, w_gate_up=w_gate_up, b_gate=b_gate, gla_norm_g=gla_norm_g,
                  w_o_gla=w_o_gla, conv_w=conv_w, w_o_conv=w_o_conv, w_out=w_out, final_norm_g=final_norm_g)
    return _run(inputs, depth=4, nseq_total=16, ntile=4, n_cores=8)
```

```python
import numpy as np
from contextlib import ExitStack
import concourse.bass as bass
import concourse.mybir as mybir
from concourse.bass_utils import run_bass_kernel_spmd

F32 = mybir.dt.float32
BF16 = mybir.dt.bfloat16
AF = mybir.ActivationFunctionType
ALU = mybir.AluOpType

D = 1024
NIN = 9232
NMETA = 16
TT = 512
EPS = 1e-6
QSCALE = 128 ** -0.5
O_Q, O_K, O_V, O_R, O_GLR, O_CH, O_CB, O_CC, O_CZ, O_GA, O_GB = 0, 512, 1024, 2048, 3072, 3088, 4112, 5136, 6160, 7184, 8208

SB = []
SB.append(("k", [("w_in", O_K, 512)]))
SB.append(("v0", [("w_in", O_V, 512)]))
SB.append(("q", [("w_in", O_Q, 512)]))
SB.append(("v1", [("w_in", O_V + 512, 512)]))
for _j in range(8):
    SB.append(("cv%d" % _j, [("w_in", O_CH + 128 * _j, 128), ("w_in", O_CC + 128 * _j, 128),
                             ("w_in", O_CB + 128 * _j, 128), ("w_in", O_CZ + 128 * _j, 128)]))
SB.append(("r0", [("w_in", O_R, 512)]))
SB.append(("r1", [("w_in", O_R + 512, 512)]))
SB.append(("wog0", [("w_o_gla", 0, 512)]))
SB.append(("wog1", [("w_o_gla", 512, 512)]))
SB.append(("ga0", [("w_in", O_GA, 512)]))
SB.append(("ga1", [("w_in", O_GA + 512, 512)]))
SB.append(("gb0", [("w_in", O_GB, 512)]))
SB.append(("gb1", [("w_in", O_GB + 512, 512)]))
SB.append(("woc0", [("w_o_conv", 0, 512)]))
SB.append(("woc1", [("w_o_conv", 512, 512)]))
SB.append(("wout0", [("w_out", 0, 512)]))
SB.append(("wout1", [("w_out", 512, 512)]))
NSB = len(SB)
SBI = {n: i for i, (n, _) in enumerate(SB)}
NW = 4


class Op:
    __slots__ = ("eng", "fn", "deps", "chan", "chan_val", "needs_sig", "sig", "idx")


class Prog:
    ENGS = ("pe", "act", "dve", "pool", "sp")

    def __init__(self):
        self.ops = {e: [] for e in self.ENGS}
        self.lastw = {}
        self.readers = {}
        self.chan_cnt = {}
        self.n = 0

    def add(self, eng, fn, reads=(), writes=(), chan=None, extra=()):
        o = Op()
        o.eng, o.fn, o.chan, o.needs_sig, o.sig, o.idx = eng, fn, chan, False, None, self.n
        self.n += 1
        o.chan_val = None
        if chan is not None:
            self.chan_cnt[chan] = self.chan_cnt.get(chan, 0) + 16
            o.chan_val = self.chan_cnt[chan]
        deps = {}
        for k in reads:
            w = self.lastw.get(k)
            if w is not None:
                deps[w.idx] = w
        for k in writes:
            w = self.lastw.get(k)
            if w is not None:
                deps[w.idx] = w
            for r in self.readers.get(k, {}).values():
                deps[r.idx] = r
        for x in extra:
            deps[x.idx] = x
        deps.pop(o.idx, None)
        o.deps = list(deps.values())
        for k in reads:
            self.readers.setdefault(k, {})[(eng, chan)] = o
        for k in writes:
            self.lastw[k] = o
            self.readers[k] = {}
        self.ops[eng].append(o)
        return o

    def emit(self, nc, stack, final_waits):
        for e in self.ENGS:
            for o in self.ops[e]:
                for p in o.deps:
                    if p.chan is None and not (p.eng == "pe" and o.eng == "pe" and o.chan is None):
                        p.needs_sig = True
        esem = {e: stack.enter_context(nc.semaphore("sem_" + e)) for e in self.ENGS}
        csem = {c: stack.enter_context(nc.semaphore("ch_" + c)) for c in self.chan_cnt}
        for e in self.ENGS:
            cnt = 0
            for o in self.ops[e]:
                if o.needs_sig:
                    cnt += 1
                    o.sig = cnt
        block = stack.enter_context(nc.Block())
        prog = self

        def run(ename, eh):
            waited = {}
            for o in prog.ops[ename]:
                for p in o.deps:
                    if p.chan is not None:
                        s, v = csem[p.chan], p.chan_val
                    else:
                        if p.eng == "pe" and o.eng == "pe" and o.chan is None:
                            continue
                        s, v = esem[p.eng], p.sig
                    key = id(s)
                    if waited.get(key, 0) >= v:
                        continue
                    waited[key] = v
                    eh.wait_ge(s, v)
                ins = o.fn(eh)
                if o.chan is not None:
                    ins.then_inc(csem[o.chan], 16)
                elif o.needs_sig:
                    ins.then_inc(esem[ename], 1)
            for c in final_waits.get(ename, ()):
                if c in csem:
                    eh.wait_ge(csem[c], prog.chan_cnt[c])

        @block.tensor
        def _(e):
            run("pe", e)

        @block.scalar
        def _(e):
            run("act", e)

        @block.vector
        def _(e):
            run("dve", e)

        @block.gpsimd
        def _(e):
            run("pool", e)

        @block.sync
        def _(e):
            run("sp", e)


def build_nc(depth, nseq, ntile, with_meta=True):
    nc = bass.Bass("TRN2", target_bir_lowering=False, dynamic_dma_scratch_size=8192)
    st = ExitStack()
    P = Prog()
    S = nseq * ntile * TT

    def dram(name, shape, dt=F32, kind="ExternalInput"):
        return nc.dram_tensor(name, list(shape), dt, kind=kind).ap()

    x_d = dram("x", [nseq, ntile * TT, D])
    meta_d = dram("meta", [NMETA, D])
    normg_d = dram("norm_g", [depth, D])
    win_d = dram("w_in", [depth, D, NIN])
    wgu_d = dram("w_gate_up", [depth, 16, 512])
    bg_d = dram("b_gate", [depth, 512])
    glag_d = dram("gla_norm_g", [depth, D])
    wog_d = dram("w_o_gla", [depth, D, D])
    convw_d = dram("conv_w", [depth, 3, D])
    woc_d = dram("w_o_conv", [depth, D, D])
    wout_d = dram("w_out", [depth, D, D])
    fng_d = dram("final_norm_g", [D])
    out_d = dram("out", [nseq, ntile * TT, D], kind="ExternalOutput")
    wb_d = dram("wb_scr", [depth, NSB, 128, 4096], BF16, kind="Internal")
    wbg_d = dram("wbg_scr", [depth, 128, 1024], BF16, kind="Internal")
    wsrc = {"w_in": win_d, "w_o_gla": wog_d, "w_o_conv": woc_d, "w_out": wout_d}

    def sb(name, shape, dt=F32):
        return st.enter_context(nc.sbuf_tensor(name, list(shape), dt))

    xT = sb("xT", [128, 8, TT])
    xTm = sb("xTm", [128, 8, NMETA])
    xin = [sb("xin%d" % i, [128, 512]) for i in range(2)]
    hT = sb("hT", [128, 8, TT], BF16)
    sqt = [sb("sqt%d" % i, [128, TT], BF16) for i in range(2)]
    rstd = sb("rstd", [128, TT])
    glr = sb("glr", [128, TT], BF16)
    scr = sb("scr", [128, 8, TT])
    ebt = [sb("ebt%d" % i, [128, TT]) for i in range(2)]
    enbt = [sb("enbt%d" % i, [128, TT]) for i in range(2)]
    kstT = [sb("kstT%d" % i, [128, TT]) for i in range(2)]
    dec = sb("dec", [128, 4, 4])
    QK = sb("QK", [128, 8, TT], BF16)
    kst = sb("kst", [128, 4, TT], BF16)
    VY = sb("VY", [128, 8, TT], BF16)
    att = [sb("att%d" % i, [128, 4, 128], BF16) for i in range(2)]
    oT = sb("oT", [128, 8, TT])
    tmpa = [sb("tmpa%d" % i, [128, TT]) for i in range(2)]
    osq = sb("osq", [128, 8, TT], BF16)
    Sst = sb("Sst", [128, depth, 4, 256])
    Smeta = sb("Smeta", [128, depth, 4, 256])
    Sbf = [sb("Sbf%d" % i, [128, 4, 256], BF16) for i in range(2)]
    ycT = sb("ycT", [128, 8, TT], BF16)
    tail = sb("tail", [128, depth, 8, 2])
    tailm = sb("tailm", [128, depth, 8, 2])
    chs = [sb("chs%d" % i, [128, TT]) for i in range(2)]
    ubuf = [sb("ubuf%d" % i, [128, TT + 2]) for i in range(2)]
    cvb = [sb("cvb%d" % i, [128, TT]) for i in range(2)]
    wring = [sb("wring%d" % i, [128, 8, 512], BF16) for i in range(NW)]
    wgl = [sb("wgl%d" % i, [128, 8, 128], BF16) for i in range(1)]
    ones_f = sb("ones_f", [128, 512])
    ones_b = sb("ones_b", [128, 128], BF16)
    ident = sb("ident", [128, 128])
    tri_i = sb("tri_i", [128, 128])
    tri_s = sb("tri_s", [128, 128])
    mask4 = sb("mask4", [128, 4, 128])
    cst_in = [sb("cst_in%d" % i, [128, 128]) for i in range(2)]
    cst = sb("cst", [128, 256])
    wgu1 = sb("wgu1", [128, depth, 512], BF16)
    epst = sb("epst", [128, 1])
    junk = sb("junk", [128, 1])
    ps = st.enter_context(nc.psum_tensor("ps", [128, 8, 512], F32))

    psn = {"c": 0, "f": 0}

    def bank(n=1, ring="c"):
        if ring == "f":
            b = 6 + psn["f"] % 2
            psn["f"] += 1
            return b
        if n == 2 and psn["c"] % 2:
            psn["c"] += 1
        b = psn["c"] % 6
        psn["c"] += n
        return b

    def PK(b, n=1):
        return [("ps", b + i) for i in range(n)]

    P.add("pool", lambda e: e.memset(ones_f[:], 1.0), writes=["ones_f"])
    P.add("pool", lambda e: e.memset(ones_b[:], 1.0), writes=["ones_b"])
    P.add("pool", lambda e: e.memset(epst[:], EPS), writes=["eps"])
    P.add("pool", lambda e: e.affine_select(out=tri_i[:], in_=ones_f[:, 0:128], pattern=[[1, 128]], compare_op=ALU.is_ge,
                                            fill=0.0, base=0, channel_multiplier=-1), reads=["ones_f"], writes=["tri_i"])
    P.add("pool", lambda e: e.affine_select(out=tri_s[:], in_=ones_f[:, 0:128], pattern=[[-1, 128]], compare_op=ALU.is_gt,
                                            fill=0.0, base=0, channel_multiplier=1), reads=["ones_f"], writes=["tri_s"])
    P.add("pool", lambda e: e.affine_select(out=ident[:], in_=ones_f[:, 0:128], pattern=[[1, 128]], compare_op=ALU.is_equal,
                                            fill=0.0, base=0, channel_multiplier=-1), reads=["ones_f"], writes=["ident"])
    P.add("pool", lambda e: e.affine_select(out=mask4[:], in_=ones_f[:].rearrange("p (a b) -> p a b", a=4),
                                            pattern=[[0, 4], [1, 128]], compare_op=ALU.is_ge, fill=0.0, base=0,
                                            channel_multiplier=-1), reads=["ones_f"], writes=["mask4"])
    P.add("pool", lambda e: e.memset(cst_in[0][:], 0.0), writes=["cst_in0"])
    P.add("pool", lambda e: e.memset(cst_in[1][:], 0.0), writes=["cst_in1"])
    P.add("pool", lambda e: e.memset(wgu1[:], 0.0), writes=["wgu1"])
    P.add("pool", lambda e: e.memset(glr[:], 1.0), writes=["glr"])
    P.add("pool", lambda e: e.memset(tail[:], 0.0), writes=[("tl", l, j) for l in range(depth) for j in range(8)])
    P.add("pool", lambda e: e.memset(Sst[:], 0.0), writes=[("S", l) for l in range(depth)])
    P.add("sp", lambda e: e.dma_start(out=cst_in[0][0:depth * 8, :], in_=normg_d.rearrange("l (j p) -> (l j) p", p=128)),
          reads=[], writes=["cst_in0"], chan="const")
    P.add("sp", lambda e: e.dma_start(out=cst_in[0][32:32 + depth * 8, :], in_=glag_d.rearrange("l (j p) -> (l j) p", p=128)),
          writes=["cst_in0"], chan="const")
    P.add("sp", lambda e: e.dma_start(out=cst_in[0][64:72, :], in_=fng_d.rearrange("(j p) -> j p", p=128)),
          writes=["cst_in0"], chan="const")
    P.add("sp", lambda e: e.dma_start(out=cst_in[1][0:depth * 24, :], in_=convw_d.rearrange("l k (j p) -> (l k j) p", p=128)),
          writes=["cst_in1"], chan="constb")
    for i in range(2):
        b = bank()
        P.add("pe", lambda e, i=i, b=b: e.transpose(out=ps[:, b, 0:128], in_=cst_in[i][:], identity=ident[:]),
              reads=["cst_in%d" % i, "ident"], writes=PK(b))
        P.add("dve", lambda e, i=i, b=b: e.tensor_copy(out=cst[:, i * 128:(i + 1) * 128], in_=ps[:, b, 0:128]),
              reads=PK(b), writes=["cst"])
    for l in range(depth):
        P.add("pool", lambda e, l=l: e.dma_start(out=wgu1[0:16, l, :], in_=wgu_d[l]), writes=["wgu1"], chan="const2")
        P.add("pool", lambda e, l=l: e.dma_start(out=wgu1[16:17, l, :], in_=bg_d[l:l + 1, :]), writes=["wgu1"], chan="const2")

    def ng(l, j):
        return cst[:, l * 8 + j:l * 8 + j + 1]

    def gg(l, j):
        return cst[:, 32 + l * 8 + j:32 + l * 8 + j + 1]

    def fg(j):
        return cst[:, 64 + j:64 + j + 1]

    def cw(l, k, j):
        c = 128 + l * 24 + k * 8 + j
        return cst[:, c:c + 1]

    cast_list = {l: [] for l in range(depth)}
    cast_last = {}
    NCC = 7
    cast_hist = []

    def cast_add(l, fn, key):
        i = len(cast_hist)
        extra = [cast_hist[i - NCC]] if i >= NCC else []
        o = P.add("pool", fn, writes=[key], chan="cast%d" % (i % NCC), extra=extra)
        cast_hist.append(o)
        cast_list[l].append(o)

    def emit_casts(l):
        cast_add(l, lambda e, l=l: e.dma_start(
            out=wbg_d[l].rearrange("p (c n) -> p c n", n=128),
            in_=win_d[l].rearrange("(c p) n -> p c n", p=128)[:, :, O_GLR:O_GLR + 128]), ("wbgc", l))
        cast_last[(l, "glr")] = 0
        for s, (name, pieces) in enumerate(SB):
            off = 0
            for pi, (src, c0, ncol) in enumerate(pieces):
                cast_add(l, lambda e, l=l, s=s, src=src, c0=c0, ncol=ncol, off=off: e.dma_start(
                    out=wb_d[l, s].rearrange("p (c n) -> p c n", n=512)[:, :, off:off + ncol],
                    in_=wsrc[src][l].rearrange("(c p) n -> p c n", p=128)[:, :, c0:c0 + ncol]), ("wbc", l, s, pi))
                off += ncol
            cast_last[(l, name)] = len(cast_list[l]) - 1

    cast_first = {}

    def cast_dep(l, name):
        lst = cast_list[l]
        last = cast_last[(l, name)]
        names = ["glr"] + [n for n, _ in SB]
        prev = names.index(name) - 1
        first = 0 if prev < 0 else cast_last[(l, names[prev])] + 1
        return lst[first:last + 1]

    for l in range(depth):
        emit_casts(l)

    wcnt = [0]

    def load_sb(l, name):
        s = SBI[name]
        slot = wcnt[0] % NW
        wcnt[0] += 1
        P.add("sp", lambda e, l=l, s=s, slot=slot: e.dma_start(
            out=wring[slot][:].rearrange("p c n -> p (c n)"), in_=wb_d[l, s]),
            writes=[("wr", slot)], chan="w%d" % slot, extra=cast_dep(l, name))
        return slot

    gcnt = [0]

    def load_glr(l):
        slot = 0
        P.add("sp", lambda e, l=l, slot=slot: e.dma_start(
            out=wgl[slot][:].rearrange("p c n -> p (c n)"), in_=wbg_d[l]),
            writes=[("wg", slot)], chan="wg%d" % slot, extra=cast_dep(l, "glr"))
        return slot

    def proj_fm(slot, m, src, skeys, T, b):
        for c in range(8):
            P.add("pe", lambda e, c=c: e.matmul(ps[:, b, 0:T], lhsT=wring[slot][:, c, m * 128:(m + 1) * 128],
                                               rhs=src[:, c, 0:T], start=(c == 0), stop=(c == 7)),
                  reads=[("wr", slot)] + ([skeys[c]] if c < len(skeys) else []) + (skeys if c == 7 else []),
                  writes=PK(b))

    def rms_stat(src, skeys, T, nch, scale, out_ap, okey, tkeys):
        b = bank()
        for i, (a, k) in enumerate(zip(src, skeys)):
            t = i % 2
            P.add("act", lambda e, a=a, t=t: e.activation(out=sqt[t][:, 0:T], in_=a, func=AF.Square),
                  reads=[k], writes=[("sqt", t)])
            P.add("pe", lambda e, t=t, i=i: e.matmul(ps[:, b, 0:T], lhsT=ones_b[:], rhs=sqt[t][:, 0:T],
                                                   start=(i == 0), stop=(i == nch - 1)),
                  reads=[("sqt", t), "ones_b"], writes=PK(b))
        P.add("act", lambda e: e.activation(out=out_ap, in_=ps[:, b, 0:T], func=AF.Ln, scale=scale, bias=epst[:]),
              reads=PK(b) + ["eps"], writes=[okey])
        P.add("act", lambda e: e.activation(out=out_ap, in_=out_ap, func=AF.Exp, scale=-0.5),
              reads=[okey], writes=[okey])

    rstd_owner = [None]

    def tile_layer(l, X, T, first_of_seq, is_meta, use_meta_state):
        chunks = [(o, min(128, T - o)) for o in range(0, T, 128)]
        nchk = len(chunks)
        XK = [("xT", j) for j in range(8)]
        HK = [("hT", j) for j in range(8)]

        if first_of_seq and not is_meta:
            if use_meta_state:
                P.add("dve", lambda e: e.tensor_copy(out=Sst[:, l], in_=Smeta[:, l]), reads=[("Sm", l)], writes=[("S", l)])
                P.add("dve", lambda e: e.tensor_copy(out=tail[:, l], in_=tailm[:, l]), reads=[("tlm", l)], writes=[("tl", l, j) for j in range(8)])
            else:
                P.add("dve", lambda e: e.memset(Sst[:, l], 0.0), writes=[("S", l)])
                P.add("dve", lambda e: e.memset(tail[:, l], 0.0), writes=[("tl", l, j) for j in range(8)])
        P.add("act", lambda e: e.activation(out=Sbf[0][:], in_=Sst[:, l], func=AF.Copy), reads=[("S", l)], writes=[("Sbf", 0)])

        if rstd_owner[0] is not X:
            rms_stat([X[:, j, 0:T] for j in range(8)], XK, T, 8, 1.0 / D, rstd[:, 0:T], "rstd", None)
            rstd_owner[0] = X
        for j in range(8):
            P.add("dve", lambda e, j=j: e.scalar_tensor_tensor(out=hT[:, j, 0:T], in0=X[:, j, 0:T], scalar=ng(l, j),
                                                              in1=rstd[:, 0:T], op0=ALU.mult, op1=ALU.mult),
                  reads=[XK[j], "rstd", "cst"], writes=[HK[j]])

        convring = ["f"]

        def conv_gen():
            for j in range(8):
                cs = load_sb(l, "cv%d" % j)
                t = j % 2
                bch = 7 if convring[0] == "f" else bank(1, "c")
                proj_fm(cs, 0, hT, HK, T, bch)
                P.add("act", lambda e, bch=bch, t=t: e.activation(out=chs[t][:, 0:T], in_=ps[:, bch, 0:T], func=AF.Copy),
                      reads=PK(bch), writes=[("chs", t)])
                yield
                bcc = 7 if convring[0] == "f" else bank(1, "c")
                proj_fm(cs, 1, hT, HK, T, bcc)
                P.add("dve", lambda e, j=j, t=t: e.tensor_copy(out=ubuf[t][:, 0:2], in_=tail[:, l, j, :]),
                      reads=[("tl", l, j)], writes=[("ub", t)])
                P.add("dve", lambda e, bcc=bcc, t=t: e.tensor_tensor(out=ubuf[t][:, 2:2 + T], in0=ps[:, bcc, 0:T], in1=chs[t][:, 0:T], op=ALU.mult),
                      reads=PK(bcc) + [("chs", t), ("ub", t)], writes=[("ub", t)])
                P.add("dve", lambda e, j=j, t=t: e.tensor_copy(out=tail[:, l, j, :], in_=ubuf[t][:, T:T + 2]),
                      reads=[("ub", t)], writes=[("tl", l, j)])
                if is_meta:
                    P.add("dve", lambda e, j=j, t=t: e.tensor_copy(out=tailm[:, l, j, :], in_=ubuf[t][:, T:T + 2]),
                          reads=[("ub", t)], writes=[("tlm", l)])
                P.add("act", lambda e, j=j, t=t: e.activation(out=cvb[t][:, 0:T], in_=ubuf[t][:, 2:2 + T], func=AF.Copy, scale=cw(l, 2, j)),
                      reads=[("ub", t), "cst"], writes=[("cvb", t)])
                P.add("dve", lambda e, j=j, t=t: e.scalar_tensor_tensor(out=cvb[t][:, 0:T], in0=ubuf[t][:, 1:1 + T], scalar=cw(l, 1, j),
                                                                       in1=cvb[t][:, 0:T], op0=ALU.mult, op1=ALU.add),
                      reads=[("ub", t), ("cvb", t)], writes=[("cvb", t)])
                P.add("dve", lambda e, j=j, t=t: e.scalar_tensor_tensor(out=cvb[t][:, 0:T], in0=ubuf[t][:, 0:T], scalar=cw(l, 0, j),
                                                                       in1=cvb[t][:, 0:T], op0=ALU.mult, op1=ALU.add),
                      reads=[("ub", t), ("cvb", t)], writes=[("cvb", t)])
                yield
                bcb = 7 if convring[0] == "f" else bank(1, "c")
                proj_fm(cs, 2, hT, HK, T, bcb)
                P.add("dve", lambda e, bcb=bcb, t=t: e.tensor_tensor(out=cvb[t][:, 0:T], in0=ps[:, bcb, 0:T], in1=cvb[t][:, 0:T], op=ALU.mult),
                      reads=PK(bcb) + [("cvb", t)], writes=[("cvb", t)])
                yield
                bcz = 7 if convring[0] == "f" else bank(1, "c")
                proj_fm(cs, 3, hT, HK, T, bcz)
                P.add("act", lambda e, bcz=bcz, t=t: e.activation(out=tmpa[t][:, 0:T], in_=ps[:, bcz, 0:T], func=AF.Silu),
                      reads=PK(bcz), writes=[("tmpa", t)])
                P.add("dve", lambda e, j=j, t=t: e.tensor_tensor(out=ycT[:, j, 0:T], in0=cvb[t][:, 0:T], in1=tmpa[t][:, 0:T], op=ALU.mult),
                      reads=[("cvb", t), ("tmpa", t)], writes=[("yc", j)])
                yield

        filler = conv_gen()

        def fill(n):
            for _ in range(n):
                try:
                    next(filler)
                except StopIteration:
                    return

        def v_chunk(vs, half, ci, o, nt):
            b = bank()
            for c in range(8):
                P.add("pe", lambda e, c=c: e.matmul(ps[0:nt, b, :], lhsT=hT[:, c, o:o + nt], rhs=wring[vs][:, c, :],
                                                   start=(c == 0), stop=(c == 7)),
                      reads=[("wr", vs), HK[c]], writes=PK(b))
            P.add("act", lambda e: e.activation(out=VY[0:nt, 2 * ci + half, :], in_=ps[0:nt, b, :], func=AF.Copy),
                  reads=PK(b), writes=[("VY", 2 * ci + half)])

        gs = load_glr(l)
        ks = load_sb(l, "k")
        v0s = load_sb(l, "v0")
        b = bank()
        for c in range(8):
            P.add("pe", lambda e, c=c, b=b: e.matmul(ps[:, b, 0:T], lhsT=wgl[gs][:, c, :], rhs=hT[:, c, 0:T],
                                                   start=(c == 0), stop=(c == 7)),
                  reads=[("wg", gs), HK[c]] + (HK if c == 7 else []), writes=PK(b))
        P.add("act", lambda e, b=b: e.activation(out=glr[0:16, 0:T], in_=ps[0:16, b, 0:T], func=AF.Copy),
              reads=PK(b), writes=["glr"])
        v_chunk(v0s, 0, 0, chunks[0][0], chunks[0][1])
        for ci, (o, nt) in enumerate(chunks):
            b = bank()
            P.add("pe", lambda e, o=o, nt=nt, b=b: e.matmul(ps[0:nt, b, :], lhsT=glr[:, o:o + nt], rhs=wgu1[:, l, :],
                                                          start=True, stop=True),
                  reads=["glr", "wgu1"], writes=PK(b))
            P.add("act", lambda e, ci=ci, nt=nt, b=b: e.activation(out=scr[0:nt, ci, :], in_=ps[0:nt, b, :], func=AF.Exp, scale=-1.0),
                  reads=PK(b), writes=[("scr", ci)])
            P.add("act", lambda e, ci=ci, nt=nt: e.activation(out=scr[0:nt, ci, :], in_=scr[0:nt, ci, :], func=AF.Ln, scale=1.0, bias=1.0),
                  reads=[("scr", ci)], writes=[("scr", ci)])
        for ci, (o, nt) in list(enumerate(chunks))[1:]:
            v_chunk(v0s, 0, ci, o, nt)
        qs = load_sb(l, "q")
        v1s = load_sb(l, "v1")
        pend_tr = []

        def emit_ktr(h, t):
            bt = bank()
            for ci, (o, nt) in enumerate(chunks):
                P.add("pe", lambda e, ci=ci, o=o, nt=nt: e.transpose(out=ps[0:nt, bt, ci * 128:(ci + 1) * 128], in_=kstT[t][:, o:o + nt],
                                                                   identity=ident[:]),
                      reads=[("kstT", t), "ident"], writes=PK(bt))
            P.add("act", lambda e: e.activation(
                out=kst[:, 0:nchk, h * 128:(h + 1) * 128], in_=ps[:, bt, 0:nchk * 128].rearrange("p (c n) -> p c n", n=128), func=AF.Copy),
                reads=PK(bt), writes=[("kst", ci) for ci in range(nchk)])

        for h in range(4):
            t = h % 2
            b = bank()
            for ci, (o, nt) in enumerate(chunks):
                P.add("pe", lambda e, ci=ci, o=o, nt=nt, b=b, h=h: e.matmul(ps[:, b, o:o + nt], lhsT=scr[0:nt, ci, h * 128:(h + 1) * 128],
                                                                         rhs=tri_i[0:nt, 0:nt], start=True, stop=True),
                      reads=[("scr", ci), "tri_i"], writes=PK(b))
            P.add("act", lambda e, b=b, t=t: e.activation(out=ebt[t][:, 0:T], in_=ps[:, b, 0:T], func=AF.Exp, scale=-1.0 / 16),
                  reads=PK(b), writes=[("eb", t)])
            P.add("act", lambda e, b=b, t=t: e.activation(out=enbt[t][:, 0:T], in_=ps[:, b, 0:T], func=AF.Exp, scale=1.0 / 16),
                  reads=PK(b), writes=[("enb", t)])
            for ci, (o, nt) in enumerate(chunks):
                P.add("dve", lambda e, h=h, ci=ci, o=o, nt=nt, t=t: e.tensor_copy(out=dec[:, h, ci:ci + 1], in_=ebt[t][:, o + nt - 1:o + nt]),
                      reads=[("eb", t)], writes=[("dec", h)])
            bk = bank()
            proj_fm(ks, h, hT, HK, T, bk)
            P.add("dve", lambda e, h=h, bk=bk, t=t: e.tensor_tensor(out=QK[:, 4 + h, 0:T], in0=ps[:, bk, 0:T], in1=enbt[t][:, 0:T], op=ALU.mult),
                  reads=PK(bk) + [("enb", t)], writes=[("QK", 4 + h)])
            if not is_meta:
                for ci, (o, nt) in enumerate(chunks):
                    P.add("dve", lambda e, h=h, bk=bk, t=t, ci=ci, o=o, nt=nt: e.scalar_tensor_tensor(
                        out=kstT[t][:, o:o + nt], in0=ps[:, bk, o:o + nt], scalar=dec[:, h, ci:ci + 1],
                        in1=enbt[t][:, o:o + nt], op0=ALU.mult, op1=ALU.mult),
                        reads=PK(bk) + [("enb", t), ("dec", h)], writes=[("kstT", t)])
                pend_tr.append((h, t))
            bq = bank()
            proj_fm(qs, h, hT, HK, T, bq)
            P.add("dve", lambda e, h=h, bq=bq, t=t: e.scalar_tensor_tensor(out=QK[:, h, 0:T], in0=ps[:, bq, 0:T], scalar=QSCALE,
                                                                        in1=ebt[t][:, 0:T], op0=ALU.mult, op1=ALU.mult),
                  reads=PK(bq) + [("eb", t)], writes=[("QK", h)])
            if h < min(2, nchk):
                v_chunk(v1s, 1, h, chunks[h][0], chunks[h][1])
            if len(pend_tr) > 1:
                emit_ktr(*pend_tr.pop(0))
        if is_meta:
            for ci, (o, nt) in enumerate(chunks):
                b2 = bank()
                for c in range(8):
                    P.add("pe", lambda e, c=c, o=o, nt=nt, b2=b2: e.matmul(ps[0:nt, b2, :], lhsT=hT[:, c, o:o + nt], rhs=wring[ks][:, c, :],
                                                                         start=(c == 0), stop=(c == 7)),
                          reads=[("wr", ks), HK[c]], writes=PK(b2))
                b = bank()
                P.add("pe", lambda e, ci=ci, nt=nt, b=b: e.matmul(ps[0:nt, b, :], lhsT=tri_s[0:nt, 0:nt], rhs=scr[0:nt, ci, :],
                                                                start=True, stop=True),
                      reads=[("scr", ci), "tri_s"], writes=PK(b))
                t = ci % 2
                P.add("act", lambda e, nt=nt, b=b, t=t: e.activation(out=tmpa[t][0:nt, :], in_=ps[0:nt, b, :], func=AF.Exp, scale=-1.0 / 16),
                      reads=PK(b), writes=[("tmpa", t)])
                P.add("dve", lambda e, ci=ci, nt=nt, b2=b2, t=t: e.tensor_tensor(out=kst[0:nt, ci, :], in0=ps[0:nt, b2, :], in1=tmpa[t][0:nt, :], op=ALU.mult),
                      reads=PK(b2) + [("tmpa", t)], writes=[("kst", ci)])
        for ci in range(2, nchk):
            if ci == 3:
                while pend_tr:
                    emit_ktr(*pend_tr.pop(0))
            v_chunk(v1s, 1, ci, chunks[ci][0], chunks[ci][1])
        while pend_tr:
            emit_ktr(*pend_tr.pop(0))

        for ci, (o, nt) in enumerate(chunks):
            a = ci % 2
            bs = 0 if a == 0 else 2
            for h in range(4):
                P.add("pe", lambda e, h=h, ci=ci, nt=nt, bs=bs: e.matmul(
                    ps[:, bs + h // 2, (h % 2) * 256:(h % 2) * 256 + 256], lhsT=kst[0:nt, ci, h * 128:(h + 1) * 128],
                    rhs=VY[0:nt, 2 * ci + h // 2, (h % 2) * 256:(h % 2) * 256 + 256], start=True, stop=True),
                    reads=[("kst", ci), ("VY", 2 * ci + h // 2)], writes=PK(bs, 2))
            fill(1)
            ba = 6
            for h in range(4):
                P.add("pe", lambda e, h=h, o=o, nt=nt, ba=ba: e.matmul(ps[0:nt, ba, h * 128:h * 128 + nt], lhsT=QK[:, 4 + h, o:o + nt],
                                                                     rhs=QK[:, h, o:o + nt], start=True, stop=True),
                      reads=[("QK", 4 + h), ("QK", h)], writes=PK(ba))
            P.add("dve", lambda e, nt=nt, ba=ba, a=a: e.tensor_tensor(
                out=att[a][0:nt, :, 0:nt], in0=ps[0:nt, ba, :].rearrange("p (h n) -> p h n", h=4)[:, :, 0:nt],
                in1=mask4[0:nt, :, 0:nt], op=ALU.mult),
                reads=PK(ba) + ["mask4"], writes=[("att", a)])
            for h in range(4):
                P.add("dve", lambda e, h=h, ci=ci, bs=bs: e.scalar_tensor_tensor(
                    out=Sst[:, l, h, :], in0=Sst[:, l, h, :], scalar=dec[:, h, ci:ci + 1],
                    in1=ps[:, bs + h // 2, (h % 2) * 256:(h % 2) * 256 + 256], op0=ALU.mult, op1=ALU.add),
                    reads=PK(bs, 2) + [("S", l), ("dec", h)], writes=[("S", l)])
            if ci < nchk - 1:
                P.add("act", lambda e, a=a: e.activation(out=Sbf[1 - a][:], in_=Sst[:, l], func=AF.Copy), reads=[("S", l)], writes=[("Sbf", 1 - a)])
            fill(1)
            bo = 4
            for h in range(4):
                for half in range(2):
                    g = 2 * h + half
                    oap = ps[:, bo + g // 4, (g % 4) * 128:(g % 4) * 128 + nt]
                    vap = VY[0:nt, 2 * ci + h // 2, (h % 2) * 256 + half * 128:(h % 2) * 256 + half * 128 + 128]
                    P.add("pe", lambda e, oap=oap, vap=vap, nt=nt, h=h, a=a: e.matmul(oap, lhsT=vap, rhs=att[a][0:nt, h, 0:nt], start=True, stop=False),
                          reads=[("VY", 2 * ci + h // 2), ("att", a)], writes=PK(bo, 2))
                    P.add("pe", lambda e, oap=oap, h=h, half=half, o=o, nt=nt, a=a: e.matmul(oap, lhsT=Sbf[a][:, h, half * 128:(half + 1) * 128],
                                                                                          rhs=QK[:, h, o:o + nt], start=False, stop=True),
                          reads=[("Sbf", a), ("QK", h)], writes=PK(bo, 2))
            P.add("act", lambda e, bo=bo, o=o, nt=nt: e.activation(
                out=osq[:, :, o:o + nt], in_=ps[:, bo:bo + 2, :].rearrange("p b (g n) -> p (b g) n", g=4)[:, :, 0:nt], func=AF.Square),
                reads=PK(bo, 2), writes=[("osq", j) for j in range(8)])
            P.add("act", lambda e, bo=bo, o=o, nt=nt: e.activation(
                out=oT[:, :, o:o + nt], in_=ps[:, bo:bo + 2, :].rearrange("p b (g n) -> p (b g) n", g=4)[:, :, 0:nt], func=AF.Copy),
                reads=PK(bo, 2), writes=[("oT", j) for j in range(8)])
        if is_meta:
            P.add("dve", lambda e: e.tensor_copy(out=Smeta[:, l], in_=Sst[:, l]), reads=[("S", l)], writes=[("Sm", l)])

        OK_ = [("oT", j) for j in range(8)]
        for h in range(4):
            b = bank()
            for half in range(2):
                P.add("pe", lambda e, b=b, h=h, half=half: e.matmul(ps[:, b, 0:T], lhsT=ones_b[:], rhs=osq[:, 2 * h + half, 0:T],
                                                                  start=(half == 0), stop=(half == 1)),
                      reads=[("osq", 2 * h + half), "ones_b"], writes=PK(b))
            P.add("act", lambda e, b=b, h=h: e.activation(out=scr[:, 4 + h, 0:T], in_=ps[:, b, 0:T], func=AF.Ln, scale=1.0 / 256, bias=epst[:]),
                  reads=PK(b) + ["eps"], writes=[("scr", 4 + h)])
        for h in range(4):
            P.add("act", lambda e, h=h: e.activation(out=scr[:, 4 + h, 0:T], in_=scr[:, 4 + h, 0:T], func=AF.Exp, scale=-0.5),
                  reads=[("scr", 4 + h)], writes=[("scr", 4 + h)])
        convring[0] = "c"
        fill(64)
        for half, nm in enumerate(("r0", "r1")):
            rs = load_sb(l, nm)
            for m in range(4):
                j = half * 4 + m
                b = bank()
                proj_fm(rs, m, hT, HK, T, b)
                t = j % 2
                P.add("act", lambda e, b=b, t=t: e.activation(out=tmpa[t][:, 0:T], in_=ps[:, b, 0:T], func=AF.Silu),
                      reads=PK(b), writes=[("tmpa", t)])
                P.add("dve", lambda e, j=j: e.scalar_tensor_tensor(out=oT[:, j, 0:T], in0=oT[:, j, 0:T], scalar=gg(l, j),
                                                                  in1=scr[:, 4 + j // 2, 0:T], op0=ALU.mult, op1=ALU.mult),
                      reads=[OK_[j], ("scr", 4 + j // 2), "cst"], writes=[OK_[j]])
                P.add("dve", lambda e, j=j, t=t: e.tensor_tensor(out=QK[:, j, 0:T], in0=oT[:, j, 0:T], in1=tmpa[t][:, 0:T], op=ALU.mult),
                      reads=[OK_[j], ("tmpa", t)], writes=[("QK", j)])
        QKK = [("QK", j) for j in range(8)]
        for half, nm in enumerate(("wog0", "wog1")):
            ws = load_sb(l, nm)
            for m in range(4):
                j = half * 4 + m
                b = bank()
                proj_fm(ws, m, QK, QKK, T, b)
                P.add("act", lambda e, b=b, j=j: e.activation(out=oT[:, j, 0:T], in_=ps[:, b, 0:T], func=AF.Copy),
                      reads=PK(b), writes=[OK_[j]])
        for half, nm in enumerate(("ga0", "ga1")):
            ws = load_sb(l, nm)
            for m in range(4):
                j = half * 4 + m
                b = bank()
                proj_fm(ws, m, hT, HK, T, b)
                t = j % 2
                P.add("act", lambda e, b=b, t=t: e.activation(out=tmpa[t][:, 0:T], in_=ps[:, b, 0:T], func=AF.Sigmoid),
                      reads=PK(b), writes=[("tmpa", t)])
                P.add("dve", lambda e, j=j, t=t: e.tensor_tensor(out=oT[:, j, 0:T], in0=oT[:, j, 0:T], in1=tmpa[t][:, 0:T], op=ALU.mult),
                      reads=[OK_[j], ("tmpa", t)], writes=[OK_[j]])
        VYK = [("yc", j) for j in range(8)]
        for half, nm in enumerate(("gb0", "gb1")):
            ws = load_sb(l, nm)
            for m in range(4):
                j = half * 4 + m
                b = bank()
                proj_fm(ws, m, hT, HK, T, b)
                P.add("act", lambda e, b=b, j=j: e.activation(out=scr[:, j, 0:T], in_=ps[:, b, 0:T], func=AF.Sigmoid),
                      reads=PK(b), writes=[("scr", j)])
        for half, nm in enumerate(("woc0", "woc1")):
            ws = load_sb(l, nm)
            for m in range(4):
                j = half * 4 + m
                b = bank()
                proj_fm(ws, m, ycT, VYK, T, b)
                P.add("dve", lambda e, b=b, j=j: e.tensor_tensor(out=scr[:, j, 0:T], in0=ps[:, b, 0:T], in1=scr[:, j, 0:T], op=ALU.mult),
                      reads=PK(b) + [("scr", j)], writes=[("scr", j)])
                P.add("dve", lambda e, j=j: e.tensor_tensor(out=hT[:, j, 0:T], in0=scr[:, j, 0:T], in1=oT[:, j, 0:T], op=ALU.add),
                      reads=[("scr", j), OK_[j]], writes=[HK[j]])
        fuse = not is_meta
        if fuse:
            bn = bank(1, "f")
            P.add("act", lambda e: e.activation(out=junk[:], in_=epst[:], func=AF.Ln), reads=["eps"], writes=["junk"])

        def onesmm(i):
            P.add("pe", lambda e: e.matmul(ps[:, bn, 0:T], lhsT=ones_b[:], rhs=sqt[i % 2][:, 0:T], start=(i == 0), stop=(i == 7)),
                  reads=[("sqt", i % 2), "ones_b"], writes=PK(bn))

        prev = None
        for half, nm in enumerate(("wout0", "wout1")):
            ws = load_sb(l, nm)
            for m in range(4):
                j = half * 4 + m
                b = bank()
                proj_fm(ws, m, hT, HK, T, b)
                P.add("dve", lambda e, b=b, j=j: e.tensor_tensor(out=X[:, j, 0:T], in0=X[:, j, 0:T], in1=ps[:, b, 0:T], op=ALU.add),
                      reads=PK(b) + [XK[j]], writes=[XK[j]])
                if fuse:
                    P.add("act", lambda e, j=j: e.activation(out=sqt[j % 2][:, 0:T], in_=X[:, j, 0:T], func=AF.Square),
                          reads=[XK[j]], writes=[("sqt", j % 2)])
                    if prev is not None:
                        onesmm(prev)
                    prev = j
        if fuse:
            onesmm(prev)
            P.add("act", lambda e: e.activation(out=rstd[:, 0:T], in_=ps[:, bn, 0:T], func=AF.Ln, scale=1.0 / D, bias=epst[:]),
                  reads=PK(bn) + ["eps"], writes=["rstd"])
            P.add("act", lambda e: e.activation(out=rstd[:, 0:T], in_=rstd[:, 0:T], func=AF.Exp, scale=-0.5),
                  reads=["rstd"], writes=["rstd"])
            rstd_owner[0] = X

    xcnt = [0]
    ocnt = [0]

    def load_tile(X, src_rows_ap, T, fuse=False):
        if fuse:
            bn = bank(1, "f")

        def emit_ones(o, nt, jj, t):
            for q in range(4):
                P.add("pe", lambda e, q=q: e.matmul(ps[:, bn, o:o + nt], lhsT=ones_b[:], rhs=sqt[t][:, q * 128:q * 128 + nt],
                                                   start=(jj == 0 and q == 0), stop=(jj == 1 and q == 3)),
                      reads=[("sqt", t), "ones_b"], writes=PK(bn))

        prev = None
        idx = 0
        for o in range(0, T, 128):
            nt = min(128, T - o)
            for jj in range(2):
                s = xcnt[0] % 2
                xcnt[0] += 1
                P.add("sp", lambda e, s=s, o=o, nt=nt, jj=jj: e.dma_start(out=xin[s][0:nt, :], in_=src_rows_ap[o:o + nt, jj * 512:(jj + 1) * 512]),
                      writes=[("xin", s)], chan="xin%d" % s)
                b = bank()
                for q in range(4):
                    P.add("pe", lambda e, s=s, nt=nt, q=q, b=b: e.transpose(out=ps[:, b, q * 128:q * 128 + nt], in_=xin[s][0:nt, q * 128:(q + 1) * 128],
                                                                          identity=ident[0:nt, 0:nt]),
                          reads=[("xin", s), "ident"], writes=PK(b))
                P.add("dve", lambda e, jj=jj, o=o, nt=nt, b=b: e.tensor_copy(
                    out=X[:, jj * 4:(jj + 1) * 4, o:o + nt], in_=ps[:, b, :].rearrange("p (q n) -> p q n", q=4)[:, :, 0:nt]),
                    reads=PK(b), writes=[("xT", j) for j in range(jj * 4, jj * 4 + 4)])
                if fuse:
                    t = idx % 2
                    idx += 1
                    P.add("act", lambda e, jj=jj, o=o, nt=nt, t=t: e.activation(
                        out=sqt[t][:, :].rearrange("p (q n) -> p q n", q=4)[:, :, 0:nt], in_=X[:, jj * 4:(jj + 1) * 4, o:o + nt], func=AF.Square),
                        reads=[("xT", j) for j in range(jj * 4, jj * 4 + 4)], writes=[("sqt", t)])
                    if prev is not None:
                        emit_ones(*prev)
                    prev = (o, nt, jj, t)
        if fuse:
            emit_ones(*prev)
            P.add("act", lambda e: e.activation(out=rstd[:, 0:T], in_=ps[:, bn, 0:T], func=AF.Ln, scale=1.0 / D, bias=epst[:]),
                  reads=PK(bn) + ["eps"], writes=["rstd"])
            P.add("act", lambda e: e.activation(out=rstd[:, 0:T], in_=rstd[:, 0:T], func=AF.Exp, scale=-0.5),
                  reads=["rstd"], writes=["rstd"])
            rstd_owner[0] = X

    def store_a(T):
        XK = [("xT", j) for j in range(8)]
        if rstd_owner[0] is not xT:
            rms_stat([xT[:, j, 0:T] for j in range(8)], XK, T, 8, 1.0 / D, rstd[:, 0:T], "rstd", None)
        rstd_owner[0] = None
        for j in range(8):
            P.add("dve", lambda e, j=j: e.scalar_tensor_tensor(out=oT[:, j, 0:T], in0=xT[:, j, 0:T], scalar=fg(j),
                                                              in1=rstd[:, 0:T], op0=ALU.mult, op1=ALU.mult),
                  reads=[XK[j], "rstd", "cst"], writes=[("oT", j)])

    def store_b(dst_rows_ap, T):
        stages = [(chs[0], ("chs", 0)), (chs[1], ("chs", 1)), (cvb[0], ("cvb", 0)), (cvb[1], ("cvb", 1))]
        for o in range(0, T, 128):
            for jj in range(2):
                si = ocnt[0] % 4
                ocnt[0] += 1
                stg, skey = stages[si]
                b = bank()
                for q in range(4):
                    j = jj * 4 + q
                    P.add("pe", lambda e, j=j, q=q, o=o, b=b: e.transpose(out=ps[:, b, q * 128:(q + 1) * 128], in_=oT[:, j, o:o + 128], identity=ident[:]),
                          reads=[("oT", j), "ident"], writes=PK(b))
                P.add("act", lambda e, stg=stg, b=b: e.activation(out=stg[:, :], in_=ps[:, b, :], func=AF.Copy),
                      reads=PK(b), writes=[skey])
                P.add("pool", lambda e, stg=stg, o=o, jj=jj: e.dma_start(out=dst_rows_ap[o:o + 128, jj * 512:(jj + 1) * 512], in_=stg[:, :]),
                      reads=[skey], writes=[], chan="out%d" % si)

    if with_meta:
        load_tile(xTm, meta_d, NMETA)
    tiles = [(sq, ti) for sq in range(nseq) for ti in range(ntile)]
    for n, (sq, ti) in enumerate(tiles):
        if n == 0:
            load_tile(xT, x_d[sq, ti * TT:(ti + 1) * TT, :], TT, fuse=True)
        for l in range(depth):
            if with_meta and sq == 0 and ti == 0:
                tile_layer(l, xTm, NMETA, True, True, False)
            tile_layer(l, xT, TT, ti == 0, False, with_meta)
        store_a(TT)
        if n + 1 < len(tiles):
            sq2, ti2 = tiles[n + 1]
            load_tile(xT, x_d[sq2, ti2 * TT:(ti2 + 1) * TT, :], TT, fuse=True)
        store_b(out_d[sq, ti * TT:(ti + 1) * TT, :], TT)

    P.emit(nc, st, {"pool": ["out0", "out1", "out2", "out3"]})
    st.close()
    return nc


def _run(inputs, depth, nseq_total, ntile, n_cores, with_meta=True):
    nseq = nseq_total // n_cores
    nc = build_nc(depth, nseq, ntile, with_meta)
    f = lambda a: np.ascontiguousarray(np.asarray(a, dtype=np.float32))
    x = f(inputs["x"])
    shared = {k: f(inputs[k]) for k in ("meta", "norm_g", "w_in", "w_gate_up", "b_gate", "gla_norm_g", "w_o_gla",
                                         "conv_w", "w_o_conv", "w_out", "final_norm_g")}
    in_maps = []
    for c in range(n_cores):
        m = dict(shared)
        m["x"] = np.ascontiguousarray(x[c * nseq:(c + 1) * nseq])
        in_maps.append(m)
    res = run_bass_kernel_spmd(nc, in_maps, core_ids=list(range(n_cores)))
    return np.concatenate([np.asarray(r["out"]) for r in res.results], axis=0).astype(np.float32)


def kernel(x, meta, norm_g, w_in, w_gate_up, b_gate, gla_norm_g, w_o_gla, conv_w, w_o_conv, w_out, final_norm_g):
    inputs = dict(x=x, meta=meta, norm_g=norm_g, w_in=w_in, w_gate_up=w_gate_up, b_gate=b_gate, gla_norm_g=gla_norm_g,
                  w_o_gla=w_o_gla, conv_w=conv_w, w_o_conv=w_o_conv, w_out=w_out, final_norm_g=final_norm_g)
    return _run(inputs, depth=4, nseq_total=16, ntile=4, n_cores=8)
```

```python
import numpy as np
from contextlib import ExitStack
import concourse.bass as bass
import concourse.mybir as mybir
from concourse.bass_utils import run_bass_kernel_spmd

F32 = mybir.dt.float32
BF16 = mybir.dt.bfloat16
AF = mybir.ActivationFunctionType
ALU = mybir.AluOpType

D = 1024
NIN = 9232
NMETA = 16
TT = 512
EPS = 1e-6
QSCALE = 128 ** -0.5
O_Q, O_K, O_V, O_R, O_GLR, O_CH, O_CB, O_CC, O_CZ, O_GA, O_GB = 0, 512, 1024, 2048, 3072, 3088, 4112, 5136, 6160, 7184, 8208

SB = []
SB.append(("k", [("w_in", O_K, 512)]))
SB.append(("v0", [("w_in", O_V, 512)]))
SB.append(("q", [("w_in", O_Q, 512)]))
SB.append(("v1", [("w_in", O_V + 512, 512)]))
for _j in range(8):
    SB.append(("cv%d" % _j, [("w_in", O_CH + 128 * _j, 128), ("w_in", O_CC + 128 * _j, 128),
                             ("w_in", O_CB + 128 * _j, 128), ("w_in", O_CZ + 128 * _j, 128)]))
SB.append(("r0", [("w_in", O_R, 512)]))
SB.append(("r1", [("w_in", O_R + 512, 512)]))
SB.append(("wog0", [("w_o_gla", 0, 512)]))
SB.append(("wog1", [("w_o_gla", 512, 512)]))
SB.append(("ga0", [("w_in", O_GA, 512)]))
SB.append(("ga1", [("w_in", O_GA + 512, 512)]))
SB.append(("gb0", [("w_in", O_GB, 512)]))
SB.append(("gb1", [("w_in", O_GB + 512, 512)]))
SB.append(("woc0", [("w_o_conv", 0, 512)]))
SB.append(("woc1", [("w_o_conv", 512, 512)]))
SB.append(("wout0", [("w_out", 0, 512)]))
SB.append(("wout1", [("w_out", 512, 512)]))
NSB = len(SB)
SBI = {n: i for i, (n, _) in enumerate(SB)}
NW = 4


class Op:
    __slots__ = ("eng", "fn", "deps", "chan", "chan_val", "needs_sig", "sig", "idx")


class Prog:
    ENGS = ("pe", "act", "dve", "pool", "sp")

    def __init__(self):
        self.ops = {e: [] for e in self.ENGS}
        self.lastw = {}
        self.readers = {}
        self.chan_cnt = {}
        self.n = 0

    def add(self, eng, fn, reads=(), writes=(), chan=None, extra=()):
        o = Op()
        o.eng, o.fn, o.chan, o.needs_sig, o.sig, o.idx = eng, fn, chan, False, None, self.n
        self.n += 1
        o.chan_val = None
        if chan is not None:
            self.chan_cnt[chan] = self.chan_cnt.get(chan, 0) + 16
            o.chan_val = self.chan_cnt[chan]
        deps = {}
        for k in reads:
            w = self.lastw.get(k)
            if w is not None:
                deps[w.idx] = w
        for k in writes:
            w = self.lastw.get(k)
            if w is not None:
                deps[w.idx] = w
            for r in self.readers.get(k, {}).values():
                deps[r.idx] = r
        for x in extra:
            deps[x.idx] = x
        deps.pop(o.idx, None)
        o.deps = list(deps.values())
        for k in reads:
            self.readers.setdefault(k, {})[(eng, chan)] = o
        for k in writes:
            self.lastw[k] = o
            self.readers[k] = {}
        self.ops[eng].append(o)
        return o

    def emit(self, nc, stack, final_waits):
        for e in self.ENGS:
            for o in self.ops[e]:
                for p in o.deps:
                    if p.chan is None and not (p.eng == "pe" and o.eng == "pe" and o.chan is None):
                        p.needs_sig = True
        esem = {e: stack.enter_context(nc.semaphore("sem_" + e)) for e in self.ENGS}
        csem = {c: stack.enter_context(nc.semaphore("ch_" + c)) for c in self.chan_cnt}
        for e in self.ENGS:
            cnt = 0
            for o in self.ops[e]:
                if o.needs_sig:
                    cnt += 1
                    o.sig = cnt
        block = stack.enter_context(nc.Block())
        prog = self

        def run(ename, eh):
            waited = {}
            for o in prog.ops[ename]:
                for p in o.deps:
                    if p.chan is not None:
                        s, v = csem[p.chan], p.chan_val
                    else:
                        if p.eng == "pe" and o.eng == "pe" and o.chan is None:
                            continue
                        s, v = esem[p.eng], p.sig
                    key = id(s)
                    if waited.get(key, 0) >= v:
                        continue
                    waited[key] = v
                    eh.wait_ge(s, v)
                ins = o.fn(eh)
                if o.chan is not None:
                    ins.then_inc(csem[o.chan], 16)
                elif o.needs_sig:
                    ins.then_inc(esem[ename], 1)
            for c in final_waits.get(ename, ()):
                if c in csem:
                    eh.wait_ge(csem[c], prog.chan_cnt[c])

        @block.tensor
        def _(e):
            run("pe", e)

        @block.scalar
        def _(e):
            run("act", e)

        @block.vector
        def _(e):
            run("dve", e)

        @block.gpsimd
        def _(e):
            run("pool", e)

        @block.sync
        def _(e):
            run("sp", e)


def build_nc(depth, nseq, ntile, with_meta=True):
    nc = bass.Bass("TRN2", target_bir_lowering=False, dynamic_dma_scratch_size=8192)
    st = ExitStack()
    P = Prog()
    S = nseq * ntile * TT

    def dram(name, shape, dt=F32, kind="ExternalInput"):
        return nc.dram_tensor(name, list(shape), dt, kind=kind).ap()

    x_d = dram("x", [nseq, ntile * TT, D])
    meta_d = dram("meta", [NMETA, D])
    normg_d = dram("norm_g", [depth, D])
    win_d = dram("w_in", [depth, D, NIN])
    wgu_d = dram("w_gate_up", [depth, 16, 512])
    bg_d = dram("b_gate", [depth, 512])
    glag_d = dram("gla_norm_g", [depth, D])
    wog_d = dram("w_o_gla", [depth, D, D])
    convw_d = dram("conv_w", [depth, 3, D])
    woc_d = dram("w_o_conv", [depth, D, D])
    wout_d = dram("w_out", [depth, D, D])
    fng_d = dram("final_norm_g", [D])
    out_d = dram("out", [nseq, ntile * TT, D], kind="ExternalOutput")
    wb_d = dram("wb_scr", [depth, NSB, 128, 4096], BF16, kind="Internal")
    wbg_d = dram("wbg_scr", [depth, 128, 1024], BF16, kind="Internal")
    wsrc = {"w_in": win_d, "w_o_gla": wog_d, "w_o_conv": woc_d, "w_out": wout_d}

    def sb(name, shape, dt=F32):
        return st.enter_context(nc.sbuf_tensor(name, list(shape), dt))

    xT = sb("xT", [128, 8, TT])
    xTm = sb("xTm", [128, 8, NMETA])
    xin = [sb("xin%d" % i, [128, 512]) for i in range(2)]
    hT = sb("hT", [128, 8, TT], BF16)
    sqt = [sb("sqt%d" % i, [128, TT], BF16) for i in range(2)]
    rstd = sb("rstd", [128, TT])
    glr = sb("glr", [128, TT], BF16)
    scr = sb("scr", [128, 8, TT])
    ebt = [sb("ebt%d" % i, [128, TT]) for i in range(2)]
    enbt = [sb("enbt%d" % i, [128, TT]) for i in range(2)]
    kstT = [sb("kstT%d" % i, [128, TT]) for i in range(2)]
    dec = sb("dec", [128, 4, 4])
    QK = sb("QK", [128, 8, TT], BF16)
    kst = sb("kst", [128, 4, TT], BF16)
    VY = sb("VY", [128, 8, TT], BF16)
    att = [sb("att%d" % i, [128, 4, 128], BF16) for i in range(2)]
    oT = sb("oT", [128, 8, TT])
    tmpa = [sb("tmpa%d" % i, [128, TT]) for i in range(2)]
    osq = sb("osq", [128, 8, TT], BF16)
    Sst = sb("Sst", [128, depth, 4, 256])
    Smeta = sb("Smeta", [128, depth, 4, 256])
    Sbf = [sb("Sbf%d" % i, [128, 4, 256], BF16) for i in range(2)]
    ycT = sb("ycT", [128, 8, TT], BF16)
    tail = sb("tail", [128, depth, 8, 2])
    tailm = sb("tailm", [128, depth, 8, 2])
    chs = [sb("chs%d" % i, [128, TT]) for i in range(2)]
    ubuf = [sb("ubuf%d" % i, [128, TT + 2]) for i in range(2)]
    cvb = [sb("cvb%d" % i, [128, TT]) for i in range(2)]
    wring = [sb("wring%d" % i, [128, 8, 512], BF16) for i in range(NW)]
    wgl = [sb("wgl%d" % i, [128, 8, 128], BF16) for i in range(1)]
    ones_f = sb("ones_f", [128, 512])
    ones_b = sb("ones_b", [128, 128], BF16)
    ident = sb("ident", [128, 128])
    tri_i = sb("tri_i", [128, 128])
    tri_s = sb("tri_s", [128, 128])
    mask4 = sb("mask4", [128, 4, 128])
    cst_in = [sb("cst_in%d" % i, [128, 128]) for i in range(2)]
    cst = sb("cst", [128, 256])
    wgu1 = sb("wgu1", [128, depth, 512], BF16)
    epst = sb("epst", [128, 1])
    junk = sb("junk", [128, 1])
    ps = st.enter_context(nc.psum_tensor("ps", [128, 8, 512], F32))

    psn = {"c": 0, "f": 0}

    def bank(n=1, ring="c"):
        if ring == "f":
            b = 6 + psn["f"] % 2
            psn["f"] += 1
            return b
        if n == 2 and psn["c"] % 2:
            psn["c"] += 1
        b = psn["c"] % 6
        psn["c"] += n
        return b

    def PK(b, n=1):
        return [("ps", b + i) for i in range(n)]

    P.add("pool", lambda e: e.memset(ones_f[:], 1.0), writes=["ones_f"])
    P.add("pool", lambda e: e.memset(ones_b[:], 1.0), writes=["ones_b"])
    P.add("pool", lambda e: e.memset(epst[:], EPS), writes=["eps"])
    P.add("pool", lambda e: e.affine_select(out=tri_i[:], in_=ones_f[:, 0:128], pattern=[[1, 128]], compare_op=ALU.is_ge,
                                            fill=0.0, base=0, channel_multiplier=-1), reads=["ones_f"], writes=["tri_i"])
    P.add("pool", lambda e: e.affine_select(out=tri_s[:], in_=ones_f[:, 0:128], pattern=[[-1, 128]], compare_op=ALU.is_gt,
                                            fill=0.0, base=0, channel_multiplier=1), reads=["ones_f"], writes=["tri_s"])
    P.add("pool", lambda e: e.affine_select(out=ident[:], in_=ones_f[:, 0:128], pattern=[[1, 128]], compare_op=ALU.is_equal,
                                            fill=0.0, base=0, channel_multiplier=-1), reads=["ones_f"], writes=["ident"])
    P.add("pool", lambda e: e.affine_select(out=mask4[:], in_=ones_f[:].rearrange("p (a b) -> p a b", a=4),
                                            pattern=[[0, 4], [1, 128]], compare_op=ALU.is_ge, fill=0.0, base=0,
                                            channel_multiplier=-1), reads=["ones_f"], writes=["mask4"])
    P.add("pool", lambda e: e.memset(cst_in[0][:], 0.0), writes=["cst_in0"])
    P.add("pool", lambda e: e.memset(cst_in[1][:], 0.0), writes=["cst_in1"])
    P.add("pool", lambda e: e.memset(wgu1[:], 0.0), writes=["wgu1"])
    P.add("pool", lambda e: e.memset(glr[:], 1.0), writes=["glr"])
    P.add("pool", lambda e: e.memset(tail[:], 0.0), writes=[("tl", l, j) for l in range(depth) for j in range(8)])
    P.add("pool", lambda e: e.memset(Sst[:], 0.0), writes=[("S", l) for l in range(depth)])
    P.add("sp", lambda e: e.dma_start(out=cst_in[0][0:depth * 8, :], in_=normg_d.rearrange("l (j p) -> (l j) p", p=128)),
          reads=[], writes=["cst_in0"], chan="const")
    P.add("sp", lambda e: e.dma_start(out=cst_in[0][32:32 + depth * 8, :], in_=glag_d.rearrange("l (j p) -> (l j) p", p=128)),
          writes=["cst_in0"], chan="const")
    P.add("sp", lambda e: e.dma_start(out=cst_in[0][64:72, :], in_=fng_d.rearrange("(j p) -> j p", p=128)),
          writes=["cst_in0"], chan="const")
    P.add("sp", lambda e: e.dma_start(out=cst_in[1][0:depth * 24, :], in_=convw_d.rearrange("l k (j p) -> (l k j) p", p=128)),
          writes=["cst_in1"], chan="constb")
    for i in range(2):
        b = bank()
        P.add("pe", lambda e, i=i, b=b: e.transpose(out=ps[:, b, 0:128], in_=cst_in[i][:], identity=ident[:]),
              reads=["cst_in%d" % i, "ident"], writes=PK(b))
        P.add("dve", lambda e, i=i, b=b: e.tensor_copy(out=cst[:, i * 128:(i + 1) * 128], in_=ps[:, b, 0:128]),
              reads=PK(b), writes=["cst"])
    for l in range(depth):
        P.add("pool", lambda e, l=l: e.dma_start(out=wgu1[0:16, l, :], in_=wgu_d[l]), writes=["wgu1"], chan="const2")
        P.add("pool", lambda e, l=l: e.dma_start(out=wgu1[16:17, l, :], in_=bg_d[l:l + 1, :]), writes=["wgu1"], chan="const2")

    def ng(l, j):
        return cst[:, l * 8 + j:l * 8 + j + 1]

    def gg(l, j):
        return cst[:, 32 + l * 8 + j:32 + l * 8 + j + 1]

    def fg(j):
        return cst[:, 64 + j:64 + j + 1]

    def cw(l, k, j):
        c = 128 + l * 24 + k * 8 + j
        return cst[:, c:c + 1]

    cast_list = {l: [] for l in range(depth)}
    cast_last = {}
    NCC = 7
    cast_hist = []

    def cast_add(l, fn, key):
        i = len(cast_hist)
        extra = [cast_hist[i - NCC]] if i >= NCC else []
        o = P.add("pool", fn, writes=[key], chan="cast%d" % (i % NCC), extra=extra)
        cast_hist.append(o)
        cast_list[l].append(o)

    def emit_casts(l):
        cast_add(l, lambda e, l=l: e.dma_start(
            out=wbg_d[l].rearrange("p (c n) -> p c n", n=128),
            in_=win_d[l].rearrange("(c p) n -> p c n", p=128)[:, :, O_GLR:O_GLR + 128]), ("wbgc", l))
        cast_last[(l, "glr")] = 0
        for s, (name, pieces) in enumerate(SB):
            off = 0
            for pi, (src, c0, ncol) in enumerate(pieces):
                cast_add(l, lambda e, l=l, s=s, src=src, c0=c0, ncol=ncol, off=off: e.dma_start(
                    out=wb_d[l, s].rearrange("p (c n) -> p c n", n=512)[:, :, off:off + ncol],
                    in_=wsrc[src][l].rearrange("(c p) n -> p c n", p=128)[:, :, c0:c0 + ncol]), ("wbc", l, s, pi))
                off += ncol
            cast_last[(l, name)] = len(cast_list[l]) - 1

    cast_first = {}

    def cast_dep(l, name):
        lst = cast_list[l]
        last = cast_last[(l, name)]
        names = ["glr"] + [n for n, _ in SB]
        prev = names.index(name) - 1
        first = 0 if prev < 0 else cast_last[(l, names[prev])] + 1
        return lst[first:last + 1]

    for l in range(depth):
        emit_casts(l)

    wcnt = [0]

    def load_sb(l, name):
        s = SBI[name]
        slot = wcnt[0] % NW
        wcnt[0] += 1
        P.add("sp", lambda e, l=l, s=s, slot=slot: e.dma_start(
            out=wring[slot][:].rearrange("p c n -> p (c n)"), in_=wb_d[l, s]),
            writes=[("wr", slot)], chan="w%d" % slot, extra=cast_dep(l, name))
        return slot

    gcnt = [0]

    def load_glr(l):
        slot = 0
        P.add("sp", lambda e, l=l, slot=slot: e.dma_start(
            out=wgl[slot][:].rearrange("p c n -> p (c n)"), in_=wbg_d[l]),
            writes=[("wg", slot)], chan="wg%d" % slot, extra=cast_dep(l, "glr"))
        return slot

    def proj_fm(slot, m, src, skeys, T, b):
        for c in range(8):
            P.add("pe", lambda e, c=c: e.matmul(ps[:, b, 0:T], lhsT=wring[slot][:, c, m * 128:(m + 1) * 128],
                                               rhs=src[:, c, 0:T], start=(c == 0), stop=(c == 7)),
                  reads=[("wr", slot)] + ([skeys[c]] if c < len(skeys) else []) + (skeys if c == 7 else []),
                  writes=PK(b))

    def rms_stat(src, skeys, T, nch, scale, out_ap, okey, tkeys):
        b = bank()
        for i, (a, k) in enumerate(zip(src, skeys)):
            t = i % 2
            P.add("act", lambda e, a=a, t=t: e.activation(out=sqt[t][:, 0:T], in_=a, func=AF.Square),
                  reads=[k], writes=[("sqt", t)])
            P.add("pe", lambda e, t=t, i=i: e.matmul(ps[:, b, 0:T], lhsT=ones_b[:], rhs=sqt[t][:, 0:T],
                                                   start=(i == 0), stop=(i == nch - 1)),
                  reads=[("sqt", t), "ones_b"], writes=PK(b))
        P.add("act", lambda e: e.activation(out=out_ap, in_=ps[:, b, 0:T], func=AF.Ln, scale=scale, bias=epst[:]),
              reads=PK(b) + ["eps"], writes=[okey])
        P.add("act", lambda e: e.activation(out=out_ap, in_=out_ap, func=AF.Exp, scale=-0.5),
              reads=[okey], writes=[okey])

    rstd_owner = [None]

    def tile_layer(l, X, T, first_of_seq, is_meta, use_meta_state, state_only=False):
        chunks = [(o, min(128, T - o)) for o in range(0, T, 128)]
        nchk = len(chunks)
        XK = [("xT", j) for j in range(8)]
        HK = [("hT", j) for j in range(8)]

        if first_of_seq and not is_meta:
            if use_meta_state:
                P.add("dve", lambda e: e.tensor_copy(out=Sst[:, l], in_=Smeta[:, l]), reads=[("Sm", l)], writes=[("S", l)])
                P.add("dve", lambda e: e.tensor_copy(out=tail[:, l], in_=tailm[:, l]), reads=[("tlm", l)], writes=[("tl", l, j) for j in range(8)])
            else:
                P.add("dve", lambda e: e.memset(Sst[:, l], 0.0), writes=[("S", l)])
                P.add("dve", lambda e: e.memset(tail[:, l], 0.0), writes=[("tl", l, j) for j in range(8)])
        P.add("act", lambda e: e.activation(out=Sbf[0][:], in_=Sst[:, l], func=AF.Copy), reads=[("S", l)], writes=[("Sbf", 0)])

        if rstd_owner[0] is not X:
            rms_stat([X[:, j, 0:T] for j in range(8)], XK, T, 8, 1.0 / D, rstd[:, 0:T], "rstd", None)
            rstd_owner[0] = X
        for j in range(8):
            P.add("dve", lambda e, j=j: e.scalar_tensor_tensor(out=hT[:, j, 0:T], in0=X[:, j, 0:T], scalar=ng(l, j),
                                                              in1=rstd[:, 0:T], op0=ALU.mult, op1=ALU.mult),
                  reads=[XK[j], "rstd", "cst"], writes=[HK[j]])

        convring = ["f"]

        def conv_gen():
            for j in range(8):
                cs = load_sb(l, "cv%d" % j)
                t = j % 2
                bch = bank(1, convring[0])
                proj_fm(cs, 0, hT, HK, T, bch)
                P.add("act", lambda e, bch=bch, t=t: e.activation(out=chs[t][:, 0:T], in_=ps[:, bch, 0:T], func=AF.Copy),
                      reads=PK(bch), writes=[("chs", t)])
                yield
                bcc = bank(1, convring[0])
                proj_fm(cs, 1, hT, HK, T, bcc)
                P.add("dve", lambda e, j=j, t=t: e.tensor_copy(out=ubuf[t][:, 0:2], in_=tail[:, l, j, :]),
                      reads=[("tl", l, j)], writes=[("ub", t)])
                P.add("dve", lambda e, bcc=bcc, t=t: e.tensor_tensor(out=ubuf[t][:, 2:2 + T], in0=ps[:, bcc, 0:T], in1=chs[t][:, 0:T], op=ALU.mult),
                      reads=PK(bcc) + [("chs", t), ("ub", t)], writes=[("ub", t)])
                P.add("dve", lambda e, j=j, t=t: e.tensor_copy(out=tail[:, l, j, :], in_=ubuf[t][:, T:T + 2]),
                      reads=[("ub", t)], writes=[("tl", l, j)])
                if is_meta:
                    P.add("dve", lambda e, j=j, t=t: e.tensor_copy(out=tailm[:, l, j, :], in_=ubuf[t][:, T:T + 2]),
                          reads=[("ub", t)], writes=[("tlm", l)])
                P.add("dve", lambda e, j=j, t=t: e.tensor_scalar(out=cvb[t][:, 0:T], in0=ubuf[t][:, 2:2 + T], scalar1=cw(l, 2, j), scalar2=None, op0=ALU.mult),
                      reads=[("ub", t), "cst"], writes=[("cvb", t)])
                P.add("dve", lambda e, j=j, t=t: e.scalar_tensor_tensor(out=cvb[t][:, 0:T], in0=ubuf[t][:, 1:1 + T], scalar=cw(l, 1, j),
                                                                       in1=cvb[t][:, 0:T], op0=ALU.mult, op1=ALU.add),
                      reads=[("ub", t), ("cvb", t)], writes=[("cvb", t)])
                P.add("dve", lambda e, j=j, t=t: e.scalar_tensor_tensor(out=cvb[t][:, 0:T], in0=ubuf[t][:, 0:T], scalar=cw(l, 0, j),
                                                                       in1=cvb[t][:, 0:T], op0=ALU.mult, op1=ALU.add),
                      reads=[("ub", t), ("cvb", t)], writes=[("cvb", t)])
                yield
                if state_only:
                    continue
                bcb = bank(1, convring[0])
                proj_fm(cs, 2, hT, HK, T, bcb)
                P.add("dve", lambda e, bcb=bcb, t=t: e.tensor_tensor(out=cvb[t][:, 0:T], in0=ps[:, bcb, 0:T], in1=cvb[t][:, 0:T], op=ALU.mult),
                      reads=PK(bcb) + [("cvb", t)], writes=[("cvb", t)])
                yield
                bcz = bank(1, convring[0])
                proj_fm(cs, 3, hT, HK, T, bcz)
                P.add("act", lambda e, bcz=bcz, t=t: e.activation(out=tmpa[t][:, 0:T], in_=ps[:, bcz, 0:T], func=AF.Silu),
                      reads=PK(bcz), writes=[("tmpa", t)])
                P.add("dve", lambda e, j=j, t=t: e.tensor_tensor(out=ycT[:, j, 0:T], in0=cvb[t][:, 0:T], in1=tmpa[t][:, 0:T], op=ALU.mult),
                      reads=[("cvb", t), ("tmpa", t)], writes=[("yc", j)])
                yield

        filler = conv_gen()

        def fill(n):
            for _ in range(n):
                try:
                    next(filler)
                except StopIteration:
                    return

        def v_chunk(vs, half, ci, o, nt):
            b = bank()
            for c in range(8):
                P.add("pe", lambda e, c=c: e.matmul(ps[0:nt, b, :], lhsT=hT[:, c, o:o + nt], rhs=wring[vs][:, c, :],
                                                   start=(c == 0), stop=(c == 7)),
                      reads=[("wr", vs), HK[c]], writes=PK(b))
            P.add("act", lambda e: e.activation(out=VY[0:nt, 2 * ci + half, :], in_=ps[0:nt, b, :], func=AF.Copy),
                  reads=PK(b), writes=[("VY", 2 * ci + half)])

        gs = load_glr(l)
        ks = load_sb(l, "k")
        v0s = load_sb(l, "v0")
        b = bank()
        for c in range(8):
            P.add("pe", lambda e, c=c, b=b: e.matmul(ps[:, b, 0:T], lhsT=wgl[gs][:, c, :], rhs=hT[:, c, 0:T],
                                                   start=(c == 0), stop=(c == 7)),
                  reads=[("wg", gs), HK[c]] + (HK if c == 7 else []), writes=PK(b))
        P.add("act", lambda e, b=b: e.activation(out=glr[0:16, 0:T], in_=ps[0:16, b, 0:T], func=AF.Copy),
              reads=PK(b), writes=["glr"])
        v_chunk(v0s, 0, 0, chunks[0][0], chunks[0][1])
        for ci, (o, nt) in enumerate(chunks):
            b = bank()
            P.add("pe", lambda e, o=o, nt=nt, b=b: e.matmul(ps[0:nt, b, :], lhsT=glr[:, o:o + nt], rhs=wgu1[:, l, :],
                                                          start=True, stop=True),
                  reads=["glr", "wgu1"], writes=PK(b))
            P.add("act", lambda e, ci=ci, nt=nt, b=b: e.activation(out=scr[0:nt, ci, :], in_=ps[0:nt, b, :], func=AF.Exp, scale=-1.0),
                  reads=PK(b), writes=[("scr", ci)])
            P.add("act", lambda e, ci=ci, nt=nt: e.activation(out=scr[0:nt, ci, :], in_=scr[0:nt, ci, :], func=AF.Ln, scale=1.0, bias=1.0),
                  reads=[("scr", ci)], writes=[("scr", ci)])
        for ci, (o, nt) in list(enumerate(chunks))[1:]:
            v_chunk(v0s, 0, ci, o, nt)
        qs = load_sb(l, "q")
        v1s = load_sb(l, "v1")
        pend_tr = []

        def emit_ktr(h, t):
            bt = bank()
            for ci, (o, nt) in enumerate(chunks):
                P.add("pe", lambda e, ci=ci, o=o, nt=nt: e.transpose(out=ps[0:nt, bt, ci * 128:(ci + 1) * 128], in_=kstT[t][:, o:o + nt],
                                                                   identity=ident[:]),
                      reads=[("kstT", t), "ident"], writes=PK(bt))
            P.add("act", lambda e: e.activation(
                out=kst[:, 0:nchk, h * 128:(h + 1) * 128], in_=ps[:, bt, 0:nchk * 128].rearrange("p (c n) -> p c n", n=128), func=AF.Copy),
                reads=PK(bt), writes=[("kst", ci) for ci in range(nchk)])

        for h in range(4):
            t = h % 2
            b = bank()
            for ci, (o, nt) in enumerate(chunks):
                P.add("pe", lambda e, ci=ci, o=o, nt=nt, b=b, h=h: e.matmul(ps[:, b, o:o + nt], lhsT=scr[0:nt, ci, h * 128:(h + 1) * 128],
                                                                         rhs=tri_i[0:nt, 0:nt], start=True, stop=True),
                      reads=[("scr", ci), "tri_i"], writes=PK(b))
            P.add("act", lambda e, b=b, t=t: e.activation(out=ebt[t][:, 0:T], in_=ps[:, b, 0:T], func=AF.Exp, scale=-1.0 / 16),
                  reads=PK(b), writes=[("eb", t)])
            P.add("act", lambda e, b=b, t=t: e.activation(out=enbt[t][:, 0:T], in_=ps[:, b, 0:T], func=AF.Exp, scale=1.0 / 16),
                  reads=PK(b), writes=[("enb", t)])
            for ci, (o, nt) in enumerate(chunks):
                P.add("dve", lambda e, h=h, ci=ci, o=o, nt=nt, t=t: e.tensor_copy(out=dec[:, h, ci:ci + 1], in_=ebt[t][:, o + nt - 1:o + nt]),
                      reads=[("eb", t)], writes=[("dec", h)])
            bk = bank()
            proj_fm(ks, h, hT, HK, T, bk)
            P.add("dve", lambda e, h=h, bk=bk, t=t: e.tensor_tensor(out=QK[:, 4 + h, 0:T], in0=ps[:, bk, 0:T], in1=enbt[t][:, 0:T], op=ALU.mult),
                  reads=PK(bk) + [("enb", t)], writes=[("QK", 4 + h)])
            if not is_meta:
                for ci, (o, nt) in enumerate(chunks):
                    P.add("dve", lambda e, h=h, bk=bk, t=t, ci=ci, o=o, nt=nt: e.scalar_tensor_tensor(
                        out=kstT[t][:, o:o + nt], in0=ps[:, bk, o:o + nt], scalar=dec[:, h, ci:ci + 1],
                        in1=enbt[t][:, o:o + nt], op0=ALU.mult, op1=ALU.mult),
                        reads=PK(bk) + [("enb", t), ("dec", h)], writes=[("kstT", t)])
                pend_tr.append((h, t))
            bq = bank()
            proj_fm(qs, h, hT, HK, T, bq)
            P.add("dve", lambda e, h=h, bq=bq, t=t: e.scalar_tensor_tensor(out=QK[:, h, 0:T], in0=ps[:, bq, 0:T], scalar=QSCALE,
                                                                        in1=ebt[t][:, 0:T], op0=ALU.mult, op1=ALU.mult),
                  reads=PK(bq) + [("eb", t)], writes=[("QK", h)])
            if h < min(2, nchk):
                v_chunk(v1s, 1, h, chunks[h][0], chunks[h][1])
            if len(pend_tr) > 1:
                emit_ktr(*pend_tr.pop(0))
        if is_meta:
            for ci, (o, nt) in enumerate(chunks):
                b2 = bank()
                for c in range(8):
                    P.add("pe", lambda e, c=c, o=o, nt=nt, b2=b2: e.matmul(ps[0:nt, b2, :], lhsT=hT[:, c, o:o + nt], rhs=wring[ks][:, c, :],
                                                                         start=(c == 0), stop=(c == 7)),
                          reads=[("wr", ks), HK[c]], writes=PK(b2))
                b = bank()
                P.add("pe", lambda e, ci=ci, nt=nt, b=b: e.matmul(ps[0:nt, b, :], lhsT=tri_s[0:nt, 0:nt], rhs=scr[0:nt, ci, :],
                                                                start=True, stop=True),
                      reads=[("scr", ci), "tri_s"], writes=PK(b))
                t = ci % 2
                P.add("act", lambda e, nt=nt, b=b, t=t: e.activation(out=tmpa[t][0:nt, :], in_=ps[0:nt, b, :], func=AF.Exp, scale=-1.0 / 16),
                      reads=PK(b), writes=[("tmpa", t)])
                P.add("dve", lambda e, ci=ci, nt=nt, b2=b2, t=t: e.tensor_tensor(out=kst[0:nt, ci, :], in0=ps[0:nt, b2, :], in1=tmpa[t][0:nt, :], op=ALU.mult),
                      reads=PK(b2) + [("tmpa", t)], writes=[("kst", ci)])
        for ci in range(2, nchk):
            if ci == 3:
                while pend_tr:
                    emit_ktr(*pend_tr.pop(0))
            v_chunk(v1s, 1, ci, chunks[ci][0], chunks[ci][1])
        while pend_tr:
            emit_ktr(*pend_tr.pop(0))

        for ci, (o, nt) in enumerate(chunks):
            a = ci % 2
            bs = bank(2)
            for h in range(4):
                P.add("pe", lambda e, h=h, ci=ci, nt=nt, bs=bs: e.matmul(
                    ps[:, bs + h // 2, (h % 2) * 256:(h % 2) * 256 + 256], lhsT=kst[0:nt, ci, h * 128:(h + 1) * 128],
                    rhs=VY[0:nt, 2 * ci + h // 2, (h % 2) * 256:(h % 2) * 256 + 256], start=True, stop=True),
                    reads=[("kst", ci), ("VY", 2 * ci + h // 2)], writes=PK(bs, 2))
            ba = bank()
            for h in range(4):
                P.add("pe", lambda e, h=h, o=o, nt=nt, ba=ba: e.matmul(ps[0:nt, ba, h * 128:h * 128 + nt], lhsT=QK[:, 4 + h, o:o + nt],
                                                                     rhs=QK[:, h, o:o + nt], start=True, stop=True),
                      reads=[("QK", 4 + h), ("QK", h)], writes=PK(ba))
            P.add("dve", lambda e, nt=nt, ba=ba, a=a: e.tensor_tensor(
                out=att[a][0:nt, :, 0:nt], in0=ps[0:nt, ba, :].rearrange("p (h n) -> p h n", h=4)[:, :, 0:nt],
                in1=mask4[0:nt, :, 0:nt], op=ALU.mult),
                reads=PK(ba) + ["mask4"], writes=[("att", a)])
            for h in range(4):
                P.add("dve", lambda e, h=h, ci=ci, bs=bs: e.scalar_tensor_tensor(
                    out=Sst[:, l, h, :], in0=Sst[:, l, h, :], scalar=dec[:, h, ci:ci + 1],
                    in1=ps[:, bs + h // 2, (h % 2) * 256:(h % 2) * 256 + 256], op0=ALU.mult, op1=ALU.add),
                    reads=PK(bs, 2) + [("S", l), ("dec", h)], writes=[("S", l)])
            if ci < nchk - 1:
                P.add("act", lambda e, a=a: e.activation(out=Sbf[1 - a][:], in_=Sst[:, l], func=AF.Copy), reads=[("S", l)], writes=[("Sbf", 1 - a)])
            fill(2)
            bo = bank(2)
            for h in range(4):
                for half in range(2):
                    g = 2 * h + half
                    oap = ps[:, bo + g // 4, (g % 4) * 128:(g % 4) * 128 + nt]
                    vap = VY[0:nt, 2 * ci + h // 2, (h % 2) * 256 + half * 128:(h % 2) * 256 + half * 128 + 128]
                    P.add("pe", lambda e, oap=oap, vap=vap, nt=nt, h=h, a=a: e.matmul(oap, lhsT=vap, rhs=att[a][0:nt, h, 0:nt], start=True, stop=False),
                          reads=[("VY", 2 * ci + h // 2), ("att", a)], writes=PK(bo, 2))
                    P.add("pe", lambda e, oap=oap, h=h, half=half, o=o, nt=nt, a=a: e.matmul(oap, lhsT=Sbf[a][:, h, half * 128:(half + 1) * 128],
                                                                                          rhs=QK[:, h, o:o + nt], start=False, stop=True),
                          reads=[("Sbf", a), ("QK", h)], writes=PK(bo, 2))
            P.add("act", lambda e, bo=bo, o=o, nt=nt: e.activation(
                out=oT[:, :, o:o + nt], in_=ps[:, bo:bo + 2, :].rearrange("p b (g n) -> p (b g) n", g=4)[:, :, 0:nt], func=AF.Copy),
                reads=PK(bo, 2), writes=[("oT", j) for j in range(8)])
            P.add("act", lambda e, bo=bo, o=o, nt=nt: e.activation(
                out=osq[:, :, o:o + nt], in_=ps[:, bo:bo + 2, :].rearrange("p b (g n) -> p (b g) n", g=4)[:, :, 0:nt], func=AF.Square),
                reads=PK(bo, 2), writes=[("osq", j) for j in range(8)])
        if is_meta:
            P.add("dve", lambda e: e.tensor_copy(out=Smeta[:, l], in_=Sst[:, l]), reads=[("S", l)], writes=[("Sm", l)])
        if state_only:
            convring[0] = "c"
            fill(64)
            return

        OK_ = [("oT", j) for j in range(8)]
        for h in range(4):
            b = bank()
            for half in range(2):
                P.add("pe", lambda e, b=b, h=h, half=half: e.matmul(ps[:, b, 0:T], lhsT=ones_b[:], rhs=osq[:, 2 * h + half, 0:T],
                                                                  start=(half == 0), stop=(half == 1)),
                      reads=[("osq", 2 * h + half), "ones_b"], writes=PK(b))
            P.add("act", lambda e, b=b, h=h: e.activation(out=scr[:, 4 + h, 0:T], in_=ps[:, b, 0:T], func=AF.Ln, scale=1.0 / 256, bias=epst[:]),
                  reads=PK(b) + ["eps"], writes=[("scr", 4 + h)])
        for h in range(4):
            P.add("act", lambda e, h=h: e.activation(out=scr[:, 4 + h, 0:T], in_=scr[:, 4 + h, 0:T], func=AF.Exp, scale=-0.5),
                  reads=[("scr", 4 + h)], writes=[("scr", 4 + h)])
        convring[0] = "c"
        fill(64)
        for half, nm in enumerate(("r0", "r1")):
            rs = load_sb(l, nm)
            for m in range(4):
                j = half * 4 + m
                b = bank()
                proj_fm(rs, m, hT, HK, T, b)
                t = j % 2
                P.add("act", lambda e, b=b, t=t: e.activation(out=tmpa[t][:, 0:T], in_=ps[:, b, 0:T], func=AF.Silu),
                      reads=PK(b), writes=[("tmpa", t)])
                P.add("dve", lambda e, j=j: e.scalar_tensor_tensor(out=oT[:, j, 0:T], in0=oT[:, j, 0:T], scalar=gg(l, j),
                                                                  in1=scr[:, 4 + j // 2, 0:T], op0=ALU.mult, op1=ALU.mult),
                      reads=[OK_[j], ("scr", 4 + j // 2), "cst"], writes=[OK_[j]])
                P.add("dve", lambda e, j=j, t=t: e.tensor_tensor(out=QK[:, j, 0:T], in0=oT[:, j, 0:T], in1=tmpa[t][:, 0:T], op=ALU.mult),
                      reads=[OK_[j], ("tmpa", t)], writes=[("QK", j)])
        QKK = [("QK", j) for j in range(8)]
        for half, nm in enumerate(("wog0", "wog1")):
            ws = load_sb(l, nm)
            for m in range(4):
                j = half * 4 + m
                b = bank()
                proj_fm(ws, m, QK, QKK, T, b)
                P.add("act", lambda e, b=b, j=j: e.activation(out=oT[:, j, 0:T], in_=ps[:, b, 0:T], func=AF.Copy),
                      reads=PK(b), writes=[OK_[j]])
        for half, nm in enumerate(("ga0", "ga1")):
            ws = load_sb(l, nm)
            for m in range(4):
                j = half * 4 + m
                b = bank()
                proj_fm(ws, m, hT, HK, T, b)
                t = j % 2
                P.add("act", lambda e, b=b, t=t: e.activation(out=tmpa[t][:, 0:T], in_=ps[:, b, 0:T], func=AF.Sigmoid),
                      reads=PK(b), writes=[("tmpa", t)])
                P.add("dve", lambda e, j=j, t=t: e.tensor_tensor(out=oT[:, j, 0:T], in0=oT[:, j, 0:T], in1=tmpa[t][:, 0:T], op=ALU.mult),
                      reads=[OK_[j], ("tmpa", t)], writes=[OK_[j]])
        VYK = [("yc", j) for j in range(8)]
        for half, nm in enumerate(("gb0", "gb1")):
            ws = load_sb(l, nm)
            for m in range(4):
                j = half * 4 + m
                b = bank()
                proj_fm(ws, m, hT, HK, T, b)
                P.add("act", lambda e, b=b, j=j: e.activation(out=scr[:, j, 0:T], in_=ps[:, b, 0:T], func=AF.Sigmoid),
                      reads=PK(b), writes=[("scr", j)])
        for half, nm in enumerate(("woc0", "woc1")):
            ws = load_sb(l, nm)
            for m in range(4):
                j = half * 4 + m
                b = bank()
                proj_fm(ws, m, ycT, VYK, T, b)
                P.add("dve", lambda e, b=b, j=j: e.tensor_tensor(out=scr[:, j, 0:T], in0=ps[:, b, 0:T], in1=scr[:, j, 0:T], op=ALU.mult),
                      reads=PK(b) + [("scr", j)], writes=[("scr", j)])
                P.add("dve", lambda e, j=j: e.tensor_tensor(out=hT[:, j, 0:T], in0=scr[:, j, 0:T], in1=oT[:, j, 0:T], op=ALU.add),
                      reads=[("scr", j), OK_[j]], writes=[HK[j]])
        fuse = not is_meta
        if fuse:
            bn = bank(1, "f")
            P.add("act", lambda e: e.activation(out=junk[:], in_=epst[:], func=AF.Ln), reads=["eps"], writes=["junk"])

        def onesmm(i):
            P.add("pe", lambda e: e.matmul(ps[:, bn, 0:T], lhsT=ones_b[:], rhs=sqt[i % 2][:, 0:T], start=(i == 0), stop=(i == 7)),
                  reads=[("sqt", i % 2), "ones_b"], writes=PK(bn))

        prev = None
        for half, nm in enumerate(("wout0", "wout1")):
            ws = load_sb(l, nm)
            for m in range(4):
                j = half * 4 + m
                b = bank()
                proj_fm(ws, m, hT, HK, T, b)
                P.add("dve", lambda e, b=b, j=j: e.tensor_tensor(out=X[:, j, 0:T], in0=X[:, j, 0:T], in1=ps[:, b, 0:T], op=ALU.add),
                      reads=PK(b) + [XK[j]], writes=[XK[j]])
                if fuse:
                    P.add("act", lambda e, j=j: e.activation(out=sqt[j % 2][:, 0:T], in_=X[:, j, 0:T], func=AF.Square),
                          reads=[XK[j]], writes=[("sqt", j % 2)])
                    if prev is not None:
                        onesmm(prev)
                    prev = j
        if fuse:
            onesmm(prev)
            P.add("act", lambda e: e.activation(out=rstd[:, 0:T], in_=ps[:, bn, 0:T], func=AF.Ln, scale=1.0 / D, bias=epst[:]),
                  reads=PK(bn) + ["eps"], writes=["rstd"])
            P.add("act", lambda e: e.activation(out=rstd[:, 0:T], in_=rstd[:, 0:T], func=AF.Exp, scale=-0.5),
                  reads=["rstd"], writes=["rstd"])
            rstd_owner[0] = X

    xcnt = [0]
    ocnt = [0]

    def load_tile(X, src_rows_ap, T, fuse=False):
        if fuse:
            bn = bank(1, "f")

        def emit_ones(o, nt, jj, t):
            for q in range(4):
                P.add("pe", lambda e, q=q: e.matmul(ps[:, bn, o:o + nt], lhsT=ones_b[:], rhs=sqt[t][:, q * 128:q * 128 + nt],
                                                   start=(jj == 0 and q == 0), stop=(jj == 1 and q == 3)),
                      reads=[("sqt", t), "ones_b"], writes=PK(bn))

        prev = None
        idx = 0
        for o in range(0, T, 128):
            nt = min(128, T - o)
            for jj in range(2):
                s = xcnt[0] % 2
                xcnt[0] += 1
                P.add("sp", lambda e, s=s, o=o, nt=nt, jj=jj: e.dma_start(out=xin[s][0:nt, :], in_=src_rows_ap[o:o + nt, jj * 512:(jj + 1) * 512]),
                      writes=[("xin", s)], chan="xin%d" % s)
                b = bank()
                for q in range(4):
                    P.add("pe", lambda e, s=s, nt=nt, q=q, b=b: e.transpose(out=ps[:, b, q * 128:q * 128 + nt], in_=xin[s][0:nt, q * 128:(q + 1) * 128],
                                                                          identity=ident[0:nt, 0:nt]),
                          reads=[("xin", s), "ident"], writes=PK(b))
                P.add("dve", lambda e, jj=jj, o=o, nt=nt, b=b: e.tensor_copy(
                    out=X[:, jj * 4:(jj + 1) * 4, o:o + nt], in_=ps[:, b, :].rearrange("p (q n) -> p q n", q=4)[:, :, 0:nt]),
                    reads=PK(b), writes=[("xT", j) for j in range(jj * 4, jj * 4 + 4)])
                if fuse:
                    t = idx % 2
                    idx += 1
                    P.add("act", lambda e, jj=jj, o=o, nt=nt, t=t: e.activation(
                        out=sqt[t][:, :].rearrange("p (q n) -> p q n", q=4)[:, :, 0:nt], in_=X[:, jj * 4:(jj + 1) * 4, o:o + nt], func=AF.Square),
                        reads=[("xT", j) for j in range(jj * 4, jj * 4 + 4)], writes=[("sqt", t)])
                    if prev is not None:
                        emit_ones(*prev)
                    prev = (o, nt, jj, t)
        if fuse:
            emit_ones(*prev)
            P.add("act", lambda e: e.activation(out=rstd[:, 0:T], in_=ps[:, bn, 0:T], func=AF.Ln, scale=1.0 / D, bias=epst[:]),
                  reads=PK(bn) + ["eps"], writes=["rstd"])
            P.add("act", lambda e: e.activation(out=rstd[:, 0:T], in_=rstd[:, 0:T], func=AF.Exp, scale=-0.5),
                  reads=["rstd"], writes=["rstd"])
            rstd_owner[0] = X

    def store_a(T):
        XK = [("xT", j) for j in range(8)]
        if rstd_owner[0] is not xT:
            rms_stat([xT[:, j, 0:T] for j in range(8)], XK, T, 8, 1.0 / D, rstd[:, 0:T], "rstd", None)
        rstd_owner[0] = None
        for j in range(8):
            P.add("dve", lambda e, j=j: e.scalar_tensor_tensor(out=oT[:, j, 0:T], in0=xT[:, j, 0:T], scalar=fg(j),
                                                              in1=rstd[:, 0:T], op0=ALU.mult, op1=ALU.mult),
                  reads=[XK[j], "rstd", "cst"], writes=[("oT", j)])

    def store_b(dst_rows_ap, T):
        stages = [(chs[0], ("chs", 0)), (chs[1], ("chs", 1)), (cvb[0], ("cvb", 0)), (cvb[1], ("cvb", 1))]
        for o in range(0, T, 128):
            for jj in range(2):
                si = ocnt[0] % 4
                ocnt[0] += 1
                stg, skey = stages[si]
                b = bank()
                for q in range(4):
                    j = jj * 4 + q
                    P.add("pe", lambda e, j=j, q=q, o=o, b=b: e.transpose(out=ps[:, b, q * 128:(q + 1) * 128], in_=oT[:, j, o:o + 128], identity=ident[:]),
                          reads=[("oT", j), "ident"], writes=PK(b))
                P.add("act", lambda e, stg=stg, b=b: e.activation(out=stg[:, :], in_=ps[:, b, :], func=AF.Copy),
                      reads=PK(b), writes=[skey])
                P.add("pool", lambda e, stg=stg, o=o, jj=jj: e.dma_start(out=dst_rows_ap[o:o + 128, jj * 512:(jj + 1) * 512], in_=stg[:, :]),
                      reads=[skey], writes=[], chan="out%d" % si)

    if with_meta:
        load_tile(xTm, meta_d, NMETA)
    tiles = [(sq, ti) for sq in range(nseq) for ti in range(ntile)]
    for n, (sq, ti) in enumerate(tiles):
        if n == 0:
            load_tile(xT, x_d[sq, ti * TT:(ti + 1) * TT, :], TT, fuse=True)
        for l in range(depth):
            if with_meta and sq == 0 and ti == 0:
                tile_layer(l, xTm, NMETA, True, True, False, state_only=(l == depth - 1))
            tile_layer(l, xT, TT, ti == 0, False, with_meta)
        store_a(TT)
        if n + 1 < len(tiles):
            sq2, ti2 = tiles[n + 1]
            load_tile(xT, x_d[sq2, ti2 * TT:(ti2 + 1) * TT, :], TT, fuse=True)
        store_b(out_d[sq, ti * TT:(ti + 1) * TT, :], TT)

    P.emit(nc, st, {"pool": ["out0", "out1", "out2", "out3"]})
    st.close()
    return nc


def _run(inputs, depth, nseq_total, ntile, n_cores, with_meta=True):
    nseq = nseq_total // n_cores
    nc = build_nc(depth, nseq, ntile, with_meta)
    f = lambda a: np.ascontiguousarray(np.asarray(a, dtype=np.float32))
    x = f(inputs["x"])
    shared = {k: f(inputs[k]) for k in ("meta", "norm_g", "w_in", "w_gate_up", "b_gate", "gla_norm_g", "w_o_gla",
                                         "conv_w", "w_o_conv", "w_out", "final_norm_g")}
    in_maps = []
    for c in range(n_cores):
        m = dict(shared)
        m["x"] = np.ascontiguousarray(x[c * nseq:(c + 1) * nseq])
        in_maps.append(m)
    res = run_bass_kernel_spmd(nc, in_maps, core_ids=list(range(n_cores)))
    return np.concatenate([np.asarray(r["out"]) for r in res.results], axis=0).astype(np.float32)


def kernel(x, meta, norm_g, w_in, w_gate_up, b_gate, gla_norm_g, w_o_gla, conv_w, w_o_conv, w_out, final_norm_g):
    inputs = dict(x=x, meta=meta, norm_g=norm_g, w_in=w_in, w_gate_up=w_gate_up, b_gate=b_gate, gla_norm_g=gla_norm_g,
                  w_o_gla=w_o_gla, conv_w=conv_w, w_o_conv=w_o_conv, w_out=w_out, final_norm_g=final_norm_g)
    return _run(inputs, depth=4, nseq_total=16, ntile=4, n_cores=8)
```
